# Optimizing a Trainium2 kernel written in Bass

```python
import math
import jax, jax.numpy as jnp
from jax import lax
import numpy as np

D_MODEL = 1024
BATCH = 16
SEQ = 4096
DEPTH = 1

CTX_LEN = 256
GRID_W = 64
MLA_HEADS = 8
NOPE_DIM = 64
ROPE_DIM = 32
V_DIM = 64
QK_DIM = NOPE_DIM + ROPE_DIM
Q_LORA = 256
KV_LORA = 128
MLA_WIDTH = MLA_HEADS * V_DIM
F_GROUPS = 4
F_GROUP_DIM = 128
F_WIDTH = F_GROUPS * F_GROUP_DIM
D_MIX = MLA_WIDTH + F_WIDTH
IN_SPLITS = (Q_LORA, KV_LORA, ROPE_DIM, MLA_WIDTH, F_WIDTH, F_WIDTH)
D_IN = sum(IN_SPLITS)
ROPE_BASE = 10000.0
Q_BLOCK = 128
LN_EPS = 1e-6
DEEPNORM_ALPHA = (2.0 * DEPTH) ** 0.25
DEEPNORM_BETA = (8.0 * DEPTH) ** -0.25

kernel_name = "hybrid_mla_fnet_prefix_block"


def _layer_norm(x, g=None, b=None):
    xf = x.astype(jnp.float32)
    mu = jnp.mean(xf, axis=-1, keepdims=True)
    var = jnp.mean(jnp.square(xf - mu), axis=-1, keepdims=True)
    y = (xf - mu) * lax.rsqrt(var + LN_EPS)
    if g is not None:
        y = y * g.astype(jnp.float32) + b.astype(jnp.float32)
    return y.astype(x.dtype)


def _rms_norm(x, g):
    xf = x.astype(jnp.float32)
    y = xf * lax.rsqrt(jnp.mean(jnp.square(xf), axis=-1, keepdims=True) + LN_EPS)
    return (y * g.astype(jnp.float32)).astype(x.dtype)


def _modulation(cvec, w_ada, b_ada):
    m = jax.nn.silu(cvec) @ w_ada + b_ada
    return jnp.split(m, 3, axis=-1)


def _axial_rope_tables(n_tokens, dtype):
    n_rows = n_tokens // GRID_W
    rows, cols = jnp.meshgrid(jnp.arange(n_rows), jnp.arange(GRID_W), indexing="ij")
    rows = rows.reshape(-1).astype(jnp.float32)
    cols = cols.reshape(-1).astype(jnp.float32)
    axis_dim = ROPE_DIM // 2
    inv_freq = ROPE_BASE ** (-jnp.arange(0, axis_dim, 2, dtype=jnp.float32) / axis_dim)
    ang = jnp.concatenate([rows[:, None] * inv_freq, cols[:, None] * inv_freq], axis=-1)
    ang = jnp.concatenate([ang, ang], axis=-1)
    return jnp.cos(ang).astype(dtype), jnp.sin(ang).astype(dtype)


def _apply_rope(x, cos, sin):
    half = x.shape[-1] // 2
    rot = jnp.concatenate([-x[..., half:], x[..., :half]], axis=-1)
    return x * cos + rot * sin


def _split_proj(h, w_in, b_in):
    proj = h @ w_in + b_in
    idx = [int(v) for v in np.cumsum(IN_SPLITS)[:-1]]
    return jnp.split(proj, idx, axis=-1)


def _mla_q(q_lat, q_norm_g, w_q_up):
    b, t, _ = q_lat.shape
    q = (_rms_norm(q_lat, q_norm_g) @ w_q_up).reshape(b, t, MLA_HEADS, QK_DIM)
    return q[..., :NOPE_DIM], q[..., NOPE_DIM:]


def _mla_kv(c_kv, kv_norm_g, w_kv_up):
    b, t, _ = c_kv.shape
    kv = (_rms_norm(c_kv, kv_norm_g) @ w_kv_up).reshape(b, t, MLA_HEADS, NOPE_DIM + V_DIM)
    return kv[..., :NOPE_DIM], kv[..., NOPE_DIM:]


def _assemble_k(k_nope, k_rope):
    b, t, h, _ = k_nope.shape
    return jnp.concatenate([k_nope, jnp.broadcast_to(k_rope[:, :, None, :], (b, t, h, ROPE_DIM))], axis=-1)


def _softmax_attend(q, k, v):
    s = jnp.einsum("bqhd,bkhd->bhqk", q, k).astype(jnp.float32) * (1.0 / math.sqrt(QK_DIM))
    p = jax.nn.softmax(s, axis=-1).astype(v.dtype)
    return jnp.einsum("bhqk,bkhv->bqhv", p, v)


def _latent_attention(q, k_lat, v_lat, k_ctx, v_ctx):
    b, s, h, d = q.shape
    k = jnp.concatenate([k_ctx, k_lat], axis=1)
    v = jnp.concatenate([v_ctx, v_lat], axis=1)
    qb = q.reshape(b, s // Q_BLOCK, Q_BLOCK, h, d).transpose(1, 0, 2, 3, 4)
    o = lax.map(lambda qi: _softmax_attend(qi, k, v), qb)
    return o.transpose(1, 0, 2, 3, 4).reshape(b, s, h * V_DIM)


def _fourier_mix(u, w_fourier, b_fourier):
    b, t, _ = u.shape
    ug = u.reshape(b, t, F_GROUPS, F_GROUP_DIM).astype(jnp.float32)
    z = jnp.fft.fft2(ug, axes=(1, 3), norm="ortho").real.astype(u.dtype).reshape(b, t, F_WIDTH)
    return z @ w_fourier + b_fourier


def _merge_out(attn, four, g_mla, g_f, w_out, b_out):
    y = jnp.concatenate([attn * jax.nn.silu(g_mla), four * jax.nn.silu(g_f)], axis=-1)
    return y @ w_out + b_out


def setup_inputs(seed: int = 0) -> dict:
    key = jax.random.key(seed)
    ks = jax.random.split(key, 20)
    f32 = jnp.float32
    nrm = lambda k, shape, s: jax.random.normal(k, shape, f32) * s
    return {
        "x": nrm(ks[0], (BATCH, SEQ, D_MODEL), 1.0),
        "c": nrm(ks[1], (BATCH, D_MODEL), 1.0),
        "ctx": nrm(ks[2], (BATCH, CTX_LEN, D_MODEL), 1.0),
        "c_ctx": nrm(ks[3], (D_MODEL,), 1.0),
        "w_ada": nrm(ks[4], (DEPTH, D_MODEL, 3 * D_MODEL), 0.5 * D_MODEL ** -0.5),
        "b_ada": nrm(ks[5], (DEPTH, 3 * D_MODEL), 0.02),
        "w_in": nrm(ks[6], (DEPTH, D_MODEL, D_IN), D_MODEL ** -0.5),
        "b_in": nrm(ks[7], (DEPTH, D_IN), 0.02),
        "q_norm_g": 1.0 + nrm(ks[8], (DEPTH, Q_LORA), 0.02),
        "w_q_up": nrm(ks[9], (DEPTH, Q_LORA, MLA_HEADS * QK_DIM), Q_LORA ** -0.5),
        "kv_norm_g": 1.0 + nrm(ks[10], (DEPTH, KV_LORA), 0.02),
        "w_kv_up": nrm(ks[11], (DEPTH, KV_LORA, MLA_HEADS * (NOPE_DIM + V_DIM)), KV_LORA ** -0.5),
        "w_fourier": nrm(ks[12], (DEPTH, F_WIDTH, F_WIDTH), F_WIDTH ** -0.5),
        "b_fourier": nrm(ks[13], (DEPTH, F_WIDTH), 0.02),
        "w_out": nrm(ks[14], (DEPTH, D_MIX, D_MODEL), DEEPNORM_BETA * D_MIX ** -0.5),
        "b_out": nrm(ks[15], (DEPTH, D_MODEL), 0.02),
        "post_ln_g": 1.0 + nrm(ks[16], (DEPTH, D_MODEL), 0.02),
        "post_ln_b": nrm(ks[17], (DEPTH, D_MODEL), 0.02),
    }


def reference(x, c, ctx, c_ctx, w_ada, b_ada, w_in, b_in, q_norm_g, w_q_up, kv_norm_g,
              w_kv_up, w_fourier, b_fourier, w_out, b_out, post_ln_g, post_ln_b):
    seq = x.shape[1]
    cos, sin = _axial_rope_tables(seq, x.dtype)
    cos_q, sin_q = cos[None, :, None, :], sin[None, :, None, :]
    cos_k, sin_k = cos[None], sin[None]
    for l in range(DEPTH):
        shift_x, scale_x, gate_x = _modulation(c, w_ada[l], b_ada[l])
        shift_c, scale_c, gate_c = _modulation(c_ctx, w_ada[l], b_ada[l])

        h_c = _layer_norm(ctx) * (1.0 + scale_c) + shift_c
        qlat_c, ckv_c, krope_c, gmla_c, fin_c, gf_c = _split_proj(h_c, w_in[l], b_in[l])
        knope_c, v_c = _mla_kv(ckv_c, kv_norm_g[l], w_kv_up[l])
        k_c = _assemble_k(knope_c, krope_c)

        h_x = _layer_norm(x) * (1.0 + scale_x[:, None, :]) + shift_x[:, None, :]
        qlat_x, ckv_x, krope_x, gmla_x, fin_x, gf_x = _split_proj(h_x, w_in[l], b_in[l])
        qnope_x, qrope_x = _mla_q(qlat_x, q_norm_g[l], w_q_up[l])
        q_x = jnp.concatenate([qnope_x, _apply_rope(qrope_x, cos_q, sin_q)], axis=-1)
        knope_x, v_x = _mla_kv(ckv_x, kv_norm_g[l], w_kv_up[l])
        k_x = _assemble_k(knope_x, _apply_rope(krope_x, cos_k, sin_k))
        attn_x = _latent_attention(q_x, k_x, v_x, k_c, v_c)
        four_x = _fourier_mix(fin_x, w_fourier[l], b_fourier[l])
        y_x = _merge_out(attn_x, four_x, gmla_x, gf_x, w_out[l], b_out[l])
        x_new = _layer_norm(DEEPNORM_ALPHA * x + gate_x[:, None, :] * y_x, post_ln_g[l], post_ln_b[l])

        if l + 1 < DEPTH:
            qnope_c, qrope_c = _mla_q(qlat_c, q_norm_g[l], w_q_up[l])
            q_c = jnp.concatenate([qnope_c, qrope_c], axis=-1)
            b, t, _ = ctx.shape
            attn_c = _softmax_attend(q_c, k_c, v_c).reshape(b, t, MLA_WIDTH)
            four_c = _fourier_mix(fin_c, w_fourier[l], b_fourier[l])
            y_c = _merge_out(attn_c, four_c, gmla_c, gf_c, w_out[l], b_out[l])
            ctx = _layer_norm(DEEPNORM_ALPHA * ctx + gate_c * y_c, post_ln_g[l], post_ln_b[l])
        x = x_new
    return x
```

```python
import contextlib
import math

import ml_dtypes
import numpy as np

import concourse.bass as bass
import concourse.mybir as mybir
from concourse.bass_utils import run_bass_kernel_spmd

F32, BF16, U8 = mybir.dt.float32, mybir.dt.bfloat16, mybir.dt.uint8
AF = mybir.ActivationFunctionType
ALU = mybir.AluOpType
NPBF = ml_dtypes.bfloat16

D = 1024
S = 4096
CTX = 256
NK = S + CTX
H = 8
EPS = 1e-6
ALPHA = 2.0 ** 0.25
SCALE = 1.0 / math.sqrt(96.0)
ARENA = 191 * 1024 + 512

SAME_SYNC = {"act", "dve", "pool"}


class Tok:
    __slots__ = ("writer", "readers")

    def __init__(self):
        self.writer = None
        self.readers = []


class Op:
    __slots__ = ("eng", "fn", "deps", "signal", "count", "chan")

    def __init__(self, eng, fn, chan=None):
        self.eng, self.fn, self.chan = eng, fn, chan
        self.deps = []
        self.signal = False
        self.count = None


class Chan:
    def __init__(self, name):
        self.name, self.sem, self.n, self.last = name, None, 0, None


class Sched:
    ENGS = ("pe", "act", "dve", "pool", "sp")

    def __init__(self):
        self.ops = {e: [] for e in self.ENGS}
        self.chans = []
        self.pending = {e: [] for e in self.ENGS}

    def chan(self, name):
        c = Chan(name)
        self.chans.append(c)
        return c

    def add(self, eng, fn, reads=(), writes=(), chan=None):
        op = Op(eng, fn, chan)
        deps = list(self.pending[eng])
        self.pending[eng] = []
        for t in reads:
            if t.writer is not None:
                deps.append(t.writer)
        for t in writes:
            if t.writer is not None:
                deps.append(t.writer)
            deps.extend(t.readers)
        for t in reads:
            t.readers.append(op)
        for t in writes:
            t.writer = op
            t.readers = []
        seen = set()
        for d in deps:
            if d is op or id(d) in seen:
                continue
            seen.add(id(d))
            op.deps.append(d)
        if chan is not None:
            chan.last = op
        self.ops[eng].append(op)
        return op

    def barrier(self):
        lasts = [self.ops[e][-1] for e in self.ENGS if self.ops[e]]
        lasts += [c.last for c in self.chans if c.last is not None]
        for e in self.ENGS:
            self.pending[e] = list(lasts)

    def finalize(self):
        for e in self.ENGS:
            for op in self.ops[e]:
                for d in op.deps:
                    if d.chan is None and not (d.eng == op.eng and d.eng not in SAME_SYNC):
                        d.signal = True
        for e in self.ENGS:
            c = 0
            for op in self.ops[e]:
                if op.chan is not None:
                    op.chan.n += 16
                    op.count = op.chan.n
                elif op.signal:
                    c += 1
                    op.count = c

    def emit(self, nc, final_waits=()):
        self.finalize()
        with contextlib.ExitStack() as st:
            esem = {e: st.enter_context(nc.semaphore("s_" + e)) for e in self.ENGS}
            for c in self.chans:
                c.sem = st.enter_context(nc.semaphore("c_" + c.name))
            blk = st.enter_context(nc.Block())

            def run(engname):
                def body(eng):
                    seen = {}
                    for op in self.ops[engname]:
                        for d in op.deps:
                            if d.chan is not None:
                                key, sem, val = ("c", id(d.chan)), d.chan.sem, d.count
                            else:
                                if d.eng == engname and engname not in SAME_SYNC:
                                    continue
                                key, sem, val = ("e", d.eng), esem[d.eng], d.count
                            if seen.get(key, 0) >= val:
                                continue
                            seen[key] = val
                            eng.wait_ge(sem, val)
                        ins = op.fn(eng)
                        if op.chan is not None:
                            ins.then_inc(op.chan.sem, 16)
                        elif op.signal:
                            ins.then_inc(esem[engname], 1)
                    if engname == "sp":
                        for c in final_waits:
                            if c.n:
                                eng.wait_ge(c.sem, c.n)

                return body

            blk.tensor(run("pe"))
            blk.scalar(run("act"))
            blk.vector(run("dve"))
            blk.gpsimd(run("pool"))
            blk.sync(run("sp"))


def build(NB, dbg=False):
    nc = bass.Bass("TRN2", target_bir_lowering=False)

    def din(name, shape, dt=F32):
        return nc.dram_tensor(name, list(shape), dt, kind="ExternalInput").ap()

    x_d = din("x", [NB, S, D])
    ctx_d = din("ctx", [NB, CTX, D])
    cT_d = din("cT", [128, 24])
    w_ada_d = din("w_ada", [D, 3 * D])
    b_ada_d = din("b_ada", [1, 3 * D])
    w_in_d = din("w_in", [D, 1952])
    bfin_d = din("b_fin", [1, 512])
    cols_d = din("cols", [128, 32])
    wq_d = din("w_q_up", [256, 768])
    wkv_d = din("w_kv_up", [128, 1024])
    w4_d = din("w_fourier", [512, 512])
    wo_d = din("w_out", [D, D])
    bo_d = din("b_out", [1, D])
    pg_d = din("post_ln_g", [1, D])
    pb_d = din("post_ln_b", [1, D])
    ident_d = din("ident", [128, 128], BF16)
    ones_d = din("ones", [128, 128])
    cs64_d = din("cs64", [64, 128], BF16)
    ff_d = din("ff", [128, 256])
    tab_d = din("tab", [8, 128, 1536], BF16)
    rope_d = din("rope", [2, 32, S])
    out_d = nc.dram_tensor("out", [NB, S, D], F32, kind="ExternalOutput").ap()
    m_scr = nc.dram_tensor("m_scr", [3, 3 * D], F32).ap()
    w_in_bf = nc.dram_tensor("w_in_bf", [D, 1952], BF16).ap()
    wq_bf = nc.dram_tensor("wq_bf", [256, 768], BF16).ap()
    wo_g = nc.dram_tensor("wo_g", [NB, D, D], BF16).ap()
    wkv_bf = nc.dram_tensor("wkv_bf", [128, 1024], BF16).ap()

    sc = Sched()
    with contextlib.ExitStack() as st:
        arena = st.enter_context(nc.sbuf_tensor("arena", [128, ARENA], U8))
        psum = st.enter_context(nc.psum_tensor("psum", [128, 4096], F32))
        ptr = [0]

        def alloc(nel, dt):
            sz = nel * (4 if dt == F32 else 2)
            off = ptr[0]
            ptr[0] = off + (sz + 63) // 64 * 64
            assert ptr[0] <= ARENA, ("arena overflow", ptr[0])
            return arena[:, off:off + sz].bitcast(dt)

        bank = [psum[:, i * 512:(i + 1) * 512] for i in range(8)]
        bankT = [Tok() for _ in range(8)]
        _ch = {}

        def ch(name):
            if name not in _ch:
                _ch[name] = sc.chan(name)
            return _ch[name]

        def mm(out, lhsT, rhs, start, stop, reads, writes):
            return sc.add("pe", lambda e: e.matmul(out, lhsT, rhs, start=start, stop=stop), reads, writes)

        def tr(out, in_, reads, writes):
            return sc.add("pe", lambda e: e.transpose(out, in_, ident), reads, writes)

        def act(out, in_, func, reads, writes, bias=None, scale=None):
            kw = {}
            if bias is not None:
                kw["bias"] = bias
            if scale is not None:
                kw["scale"] = scale
            return sc.add("act", lambda e: e.activation(out, in_, func, **kw), reads, writes)

        def tt(eng, out, a, b, op, reads, writes):
            return sc.add(eng, lambda e: e.tensor_tensor(out, a, b, op), reads, writes)

        def ts(eng, out, a, s1, s2, op0, op1, reads, writes):
            if op1 is None:
                return sc.add(eng, lambda e: e.tensor_scalar(out, a, s1, None, op0), reads, writes)
            return sc.add(eng, lambda e: e.tensor_scalar(out, a, s1, s2, op0, op1), reads, writes)

        def stt(eng, out, a, s, b, op0, op1, reads, writes):
            return sc.add(eng, lambda e: e.scalar_tensor_tensor(out, a, s, b, op0, op1), reads, writes)

        def cp(eng, out, in_, reads, writes):
            if eng == "act":
                return sc.add(eng, lambda e: e.copy(out, in_), reads, writes)
            return sc.add(eng, lambda e: e.tensor_copy(out, in_), reads, writes)

        def ms(eng, out, val, reads, writes):
            return sc.add(eng, lambda e: e.memset(out, val), reads, writes)

        def dma(q, chname, out, in_, reads=(), writes=()):
            return sc.add(q, lambda e: e.dma_start(out=out, in_=in_), reads, writes, chan=ch(chname))

        ident = alloc(128, BF16)
        ones = alloc(128, F32)
        cs64 = alloc(128, BF16)
        ff = alloc(256, F32)
        cols = alloc(32, F32)
        bfin = alloc(512, F32)
        epsc = alloc(1, F32)
        mhalf = alloc(1, F32)
        R1 = alloc(4 * S, BF16).rearrange("p (j t) -> p j t", j=4)
        R2 = alloc(4 * S, BF16).rearrange("p (j t) -> p j t", j=4)
        tC = Tok()
        dma("sp", "c0", ident, ident_d, writes=[tC])
        dma("sp", "c1", ones, ones_d, writes=[tC])
        dma("sp", "c2", cs64[0:64, :], cs64_d, writes=[tC])
        dma("sp", "c3", ff, ff_d, writes=[tC])
        dma("sp", "c4", cols, cols_d, writes=[tC])
        dma("sp", "c5", bfin, bfin_d.partition_broadcast(128), writes=[tC])
        ms("dve", epsc, EPS, [], [tC])
        ms("dve", mhalf, -0.5, [], [tC])
        tt("dve", cols[:, 20:23], cols[:, 0:3], cols[:, 13:16], ALU.mult, [tC], [tC])
        base0 = ptr[0]
        tWbf = Tok()
        tWbfA = Tok()
        dma("pool", "cvt0a", w_in_bf[:, 928:1952], w_in_d[:, 928:1952], writes=[tWbfA])
        dma("pool", "cvt0", w_in_bf[:, 0:928], w_in_d[:, 0:928], writes=[tWbf])
        dma("pool", "cvt1", wq_bf, wq_d, writes=[tWbf])
        dma("pool", "cvt2", wkv_bf, wkv_d, writes=[tWbf])

        sct = alloc(24, F32)
        bada = alloc(3 * D, F32)
        mrow = alloc(3 * D, F32)
        wst = [alloc(8 * 512, F32).rearrange("p (k n) -> p k n", k=8) for _ in range(2)]
        wstT = [Tok(), Tok()]
        t0 = Tok()
        dma("sp", "p0a", sct, cT_d, writes=[t0])
        dma("sp", "p0b", bada[0:3, :], b_ada_d.partition_broadcast(3), writes=[t0])
        act(sct, sct, AF.Silu, [t0], [t0])
        wa_v = w_ada_d.rearrange("(k p) n -> p k n", p=128)
        tm = Tok()
        for n in range(6):
            sl = n % 2
            dma("sp" if sl == 0 else "act", "wst%d" % sl, wst[sl], wa_v[:, :, n * 512:(n + 1) * 512], writes=[wstT[sl]])
            for k in range(8):
                mm(bank[n % 2][0:3, :], sct[:, k * 3:(k + 1) * 3], wst[sl][:, k, :], k == 0, k == 7,
                   [t0, wstT[sl]], [bankT[n % 2]])
            tt("dve", mrow[0:3, n * 512:(n + 1) * 512], bank[n % 2][0:3, :], bada[0:3, n * 512:(n + 1) * 512],
               ALU.add, [bankT[n % 2], t0], [tm])
        tscr = Tok()
        dma("sp", "mscr", m_scr, mrow[0:3, :], reads=[tm], writes=[tscr])
        sc.barrier()

        def ln_common_alloc(nxt, nhb):
            d = {}
            d["nxt"], d["nhb"] = nxt, nhb
            d["A"] = alloc(D, F32)
            d["B"] = alloc(D, F32)
            d["xt"] = [alloc(D, F32) for _ in range(nxt)]
            d["hb"] = [alloc(D, BF16) for _ in range(nhb)]
            d["hT"] = [alloc(8 * 512, BF16).rearrange("p (k t) -> p k t", k=8) for _ in range(2)]
            d["st"] = [alloc(12, F32) for _ in range(nxt)]
            d["mv"] = [alloc(2, F32) for _ in range(nxt)]
            d["rs"] = [alloc(1, F32) for _ in range(nxt)]
            d["xtT"] = [Tok() for _ in range(nxt)]
            d["hbT"] = [Tok() for _ in range(nhb)]
            d["hTT"] = [Tok(), Tok()]
            d["stT"] = [Tok() for _ in range(nxt)]
            d["abT"] = Tok()
            return d

        def load_ab(L, row):
            dma("sp", "abA", L["A"], m_scr[row:row + 1, D:2 * D].partition_broadcast(128), reads=[tscr], writes=[L["abT"]])
            dma("sp", "abB", L["B"], m_scr[row:row + 1, 0:D].partition_broadcast(128), reads=[tscr], writes=[L["abT"]])
            ts("dve", L["A"], L["A"], 1.0, None, ALU.add, None, [L["abT"]], [L["abT"]])

        def ln_load(L, n, srcs):
            i = n % L["nxt"]
            for (p0, p1, ap) in srcs:
                dma("sp", "xt%d_%d" % (i, p0), L["xt"][i][p0:p1, :], ap, writes=[L["xtT"][i]])

        def ln_a(L, n, srcs):
            i = n % L["nxt"]
            xt, xtT, stT = L["xt"][i], L["xtT"][i], L["stT"][i]
            st_, mv_, rs_ = L["st"][i], L["mv"][i], L["rs"][i]
            sc.add("dve", lambda e: e.bn_stats(st_[:, 0:6], xt[:, 0:512]), [xtT], [stT])
            sc.add("dve", lambda e: e.bn_stats(st_[:, 6:12], xt[:, 512:1024]), [xtT], [stT])
            sc.add("dve", lambda e: e.bn_aggr(mv_, st_), [stT], [stT])
            tt("pool", rs_, mv_[:, 1:2], epsc[:, 0:1], ALU.add, [stT, tC], [stT])
            tt("pool", rs_, rs_, mhalf[:, 0:1], ALU.pow, [stT, tC], [stT])

        def ln_b(L, n, hs, ti):
            i = n % L["nxt"]
            xt, xtT, stT = L["xt"][i], L["xtT"][i], L["stT"][i]
            mv_, rs_ = L["mv"][i], L["rs"][i]
            stt("dve", xt, xt, mv_[:, 0:1], L["A"], ALU.subtract, ALU.mult, [xtT, stT, L["abT"]], [xtT])
            hb, hbT = L["hb"][n % L["nhb"]], L["hbT"][n % L["nhb"]]
            stt("dve", hb, xt, rs_[:, 0:1], L["B"], ALU.mult, ALU.add, [xtT, stT, L["abT"]], [hbT])
            pbk = 7
            pT = bank[pbk].bitcast(BF16).rearrange("p (k t) -> p k t", k=8)
            for k in range(8):
                tr(pT[:, k, :], hb[:, k * 128:(k + 1) * 128], [hbT, tC], [bankT[pbk]])
            cp("act", L["hT"][hs][:, :, ti * 128:(ti + 1) * 128], pT, [bankT[pbk]], [L["hTT"][hs]])

        def ln_pipeline(L, tiles, chunk_gen, per):
            pend = None
            ahead = L["nxt"] - 1
            for n in range(min(ahead, len(tiles))):
                ln_load(L, n, tiles[n][0])
            ln_a(L, 0, tiles[0][0])
            for n, (srcs, hs, ti, last, pre) in enumerate(tiles):
                if pre is not None:
                    pre()
                if n + ahead < len(tiles):
                    ln_load(L, n + ahead, tiles[n + ahead][0])
                if n + 1 < len(tiles):
                    ln_a(L, n + 1, tiles[n + 1][0])
                ln_b(L, n, hs, ti)
                if pend is not None:
                    for _ in range(per):
                        next(pend, None)
                if last is not None:
                    if pend is not None:
                        for _ in pend:
                            pass
                    pend = chunk_gen(last, hs)
            if pend is not None:
                for _ in pend:
                    pass

        w_in_v = w_in_bf.rearrange("(k p) n -> p k n", p=128)
        rr = [0]

        def nb(lst):
            rr[0] += 1
            return lst[rr[0] % len(lst)]

        for b in range(NB):
            ptr[0] = base0
            X = alloc(2 * 64 * 256, BF16)
            baseX = ptr[0]
            L = ln_common_alloc(3, 1)
            WA1 = alloc(8 * 1024, BF16).rearrange("p (k n) -> p k n", k=8)
            Xs = [alloc(512, BF16) for _ in range(2)]
            XsT = [Tok(), Tok()]
            tW = Tok()
            dma("sp", "wa1", WA1, w_in_v[:, :, 928:1952], reads=[tWbfA], writes=[tW])
            load_ab(L, b)
            xperm = x_d[b].rearrange("(t1 a j) d -> a j t1 d", a=32, j=2)
            Xv = X.rearrange("p (h t w) -> p h t w", h=2, t=64)

            def a1_chunk(c, hs):
                hT, hTT = L["hT"][hs], L["hTT"][hs]
                for ti in range(4):
                    t2 = 8 * c + 2 * ti
                    bkf = nb([0, 1, 2, 3, 4, 5])
                    for k in range(8):
                        mm(bank[bkf], hT[:, k, ti * 128:(ti + 1) * 128], WA1[:, k, 0:512], k == 0, k == 7, [tW, hTT], [bankT[bkf]])
                    yield
                    j = ti
                    bk = nb([0, 1, 2, 3, 4, 5])
                    for k in range(8):
                        mm(bank[bk], WA1[:, k, 512 + j * 128:512 + (j + 1) * 128], hT[:, k, :], k == 0, k == 7,
                           [tW, hTT], [bankT[bk]])
                    ov = R1[:, j, :].rearrange("p (t1 c ti j) -> p c ti j t1", c=8, ti=4, j=2)[:, c]
                    act(ov, bank[bk].rearrange("p (ti j t1) -> p ti j t1", ti=4, j=2), AF.Silu, [bankT[bk], tC], [],
                        bias=cols[:, 9 + j:10 + j])
                    xs_, xsT_ = Xs[(4 * c + ti) % 2], XsT[(4 * c + ti) % 2]
                    tt("dve", xs_, bank[bkf], bfin, ALU.add, [bankT[bkf], tC], [xsT_])
                    for jj in range(2):
                        dma("sp", "xs%d_%d" % ((4 * c + ti) % 2, jj), Xv[0:64, :, t2 + jj, :],
                            xs_[64 * jj:64 * jj + 64, :].rearrange("p (h w) -> p h w", h=2), reads=[xsT_], writes=[])
                    yield

            tiles = []
            for c in range(8):
                for ti in range(4):
                    a = 4 * c + ti
                    tiles.append(([(0, 64, xperm[a, 0]), (64, 128, xperm[a, 1])], c % 2, ti, c if ti == 3 else None, None))
            ln_pipeline(L, tiles, a1_chunk, 2)
            sc.barrier()

            ptr[0] = baseX
            W4s = alloc(4 * 512, F32).rearrange("p (g n) -> p g n", g=4)
            W4p = alloc(8 * 512, BF16).rearrange("p (j n) -> p j n", j=8)
            G = R2.rearrange("p j t -> p (j t)")
            GT = Tok()
            tabs = [alloc(1536, BF16) for _ in range(2)]
            tabT = [Tok(), Tok()]
            Ych = [alloc(8 * 512, BF16) for _ in range(2)]
            YchT = [Tok(), Tok()]
            gate_bc = alloc(D, F32)
            wof = [alloc(D, F32) for _ in range(2)]
            wog = [alloc(D, BF16) for _ in range(2)]
            wofT = [Tok(), Tok()]
            wogT = [Tok(), Tok()]
            tG = Tok()
            dma("sp", "gatebc", gate_bc, m_scr[b:b + 1, 2 * D:3 * D].partition_broadcast(128), reads=[tscr], writes=[tG])

            def fold_chunk(k):
                sl = k % 2
                dma("sp", "wof%d" % sl, wof[sl], wo_d[k * 128:(k + 1) * 128, :], writes=[wofT[sl]])
                tt("pool", wog[sl], wof[sl], gate_bc, ALU.mult, [wofT[sl], tG], [wogT[sl]])
                dma("pool", "wog%d" % sl, wo_g[b, k * 128:(k + 1) * 128, :], wog[sl], reads=[wogT[sl]], writes=[wogT[sl]])

            tW4 = Tok()
            dma("sp", "w4s", W4s, w4_d.rearrange("(g p) n -> p g n", p=128), writes=[tW4])
            tW4p = Tok()
            for g in range(4):
                for cs in range(2):
                    bk = nb([0, 1, 2, 3])
                    mm(bank[bk], ff[:, cs * 128:(cs + 1) * 128], W4s[:, g, :], True, True, [tW4, tC], [bankT[bk]])
                    cp("dve", W4p[:, g * 2 + cs, :], bank[bk], [bankT[bk]], [tW4p])
            for kh in range(2):
                for c0 in range(0, 256, 8):
                    bk = nb([0, 1, 2, 3])
                    for cw in range(c0, c0 + 8):
                        mm(bank[bk][:, (cw - c0) * 64:(cw - c0 + 1) * 64], X[0:64, cw:cw + 127 * 256 + 1:256],
                           cs64[0:64, kh * 64:(kh + 1) * 64], True, True, [tC], [bankT[bk]])
                    cp("act" if (c0 // 8) % 2 else "dve", G[:, c0 * 64:(c0 + 8) * 64], bank[bk], [bankT[bk]], [GT])
                for kc4 in range(4):
                    kc = kh * 4 + kc4
                    sl = kc % 2
                    dma("sp", "tab%d" % sl, tabs[sl], tab_d[kc], writes=[tabT[sl]])
                    fold_chunk(kc)
                    Yv = Ych[sl].rearrange("p (g c k q) -> p g c k q", g=4, c=2, k=8)
                    for g in range(4):
                        chf, gg = g // 2, g % 2
                        for half in range(2):
                            bk = nb([4, 5, 6, 7])
                            for k1i in range(4):
                                k1l = half * 4 + k1i
                                k1h = kc4 * 8 + k1l
                                for cs in range(2):
                                    o0 = gg * 128 * 64 + cs * 32 + k1h
                                    to = k1l * 192 + (64 if cs == 0 else 0)
                                    mm(bank[bk][:, k1i * 128:(k1i + 1) * 128], G[64 * chf:64 * chf + 64, o0:o0 + 127 * 64 + 1:64],
                                       tabs[sl][64 * chf:64 * chf + 64, to:to + 128], cs == 0, cs == 1, [GT, tabT[sl]], [bankT[bk]])
                            cp("act" if half else "dve", Yv[:, g, :, half * 4:(half + 1) * 4, :],
                               bank[bk].rearrange("p (k c q) -> p c k q", k=4, c=2), [bankT[bk]], [YchT[sl]])
                    Yf = Ych[sl].rearrange("p (j t) -> p j t", j=8)
                    for nt in range(4):
                        bk = nb([0, 1, 2, 3])
                        for j in range(8):
                            mm(bank[bk], W4p[:, j, nt * 128:(nt + 1) * 128], Yf[:, j, :], j == 0, j == 7, [tW4p, YchT[sl]], [bankT[bk]])
                        rv = R1[:, nt, :].rearrange("p (k2 k1) -> p k1 k2", k1=64)[:, kc * 8:(kc + 1) * 8, :]
                        stt("dve", rv, bank[bk].rearrange("p (k q) -> p k q", k=8), cols[:, 16 + nt:17 + nt], rv, ALU.add, ALU.mult,
                            [bankT[bk], tC], [])
            sc.barrier()

            ptr[0] = base0
            qlatn = alloc(2 * S, BF16).rearrange("p (j t) -> p j t", j=2)
            ckvn = alloc(NK, BF16)
            Kb = [alloc(NK, BF16) for _ in range(2)]
            baseAt = ptr[0]
            L = ln_common_alloc(3, 2)
            WA2 = alloc(8 * 1280, BF16).rearrange("p (k n) -> p k n", k=8)
            rp = alloc(2 * 512, F32).rearrange("p (i t) -> p i t", i=2)
            rpT = Tok()
            sq = [alloc(512, F32) for _ in range(2)]
            sqT = [Tok(), Tok()]
            rq = alloc(512, F32)
            rqT = Tok()
            qg = [alloc(512, F32) for _ in range(3)]
            qgT = [Tok() for _ in range(3)]
            rq2 = alloc(512, F32)
            rq2T = Tok()
            kt1, kt2 = sq[0], sq[1]
            tW = Tok()
            ms("pool", WA2[:, :, 384:640], 0.0, [], [tW])
            dma("sp", "wa2a", WA2[:, :, 0:384], w_in_v[:, :, 0:384], reads=[tWbf], writes=[tW])
            dma("sp", "wa2b", WA2[:, :, 640:1152], w_in_v[:, :, 416:928], reads=[tWbf], writes=[tW])
            dma("sp", "wa2c", WA2[:, :, 384 + 64:384 + 96], w_in_v[:, :, 384:416], reads=[tWbf], writes=[tW])
            dma("sp", "wa2d", WA2[:, :, 512 + 64:512 + 80], w_in_v[:, :, 400:416], reads=[tWbf], writes=[tW])
            dma("sp", "wa2e", WA2[:, :, 512 + 80:512 + 96], w_in_v[:, :, 384:400], reads=[tWbf], writes=[tW])
            rope_v = rope_d.rearrange("i p t -> p i t")

            def rms_front(pbs, nfeat, gcol, bcol, bgcol, n, rq_, rqT_, qgi):
                pss = nb([4, 5])
                for j, pb in enumerate(pbs):
                    act(sq[j][:, 0:n], bank[pb][:, 0:n], AF.Square, [bankT[pb], tC], [sqT[j]], bias=cols[:, bcol + j:bcol + j + 1])
                    mm(bank[pss][:, 0:n], ones, sq[j][:, 0:n], j == 0, j == len(pbs) - 1, [sqT[j], tC], [bankT[pss]])
                act(rq_[:, 0:n], bank[pss][:, 0:n], AF.Ln, [bankT[pss]], [rqT_], bias=epsc[:, 0:1], scale=1.0 / nfeat)
                act(rq_[:, 0:n], rq_[:, 0:n], AF.Exp, [rqT_], [rqT_], scale=-0.5)
                for j, pb in enumerate(pbs):
                    act(qg[qgi + j][:, 0:n], bank[pb][:, 0:n], AF.Identity, [bankT[pb], tC], [qgT[qgi + j]],
                        bias=cols[:, bgcol + j:bgcol + j + 1], scale=cols[:, gcol + j:gcol + j + 1])

            def rms_back(outs, n, rq_, rqT_, qgi):
                for j, o in enumerate(outs):
                    tt("dve", o, qg[qgi + j][:, 0:n], rq_[:, 0:n], ALU.mult, [qgT[qgi + j], rqT_], [])

            def proj(tile_idx, bk, n, hs):
                for k in range(8):
                    mm(bank[bk][:, 0:n], WA2[:, k, tile_idx * 128:(tile_idx + 1) * 128], L["hT"][hs][:, k, 0:n], k == 0, k == 7,
                       [tW, L["hTT"][hs]], [bankT[bk]])

            def a2_chunk(c, hs):
                if c < 0:
                    proj(2, 0, 256, hs)
                    rms_front([0], 128.0, 15, 2, 22, 256, rq2, rq2T, 2)
                    proj(3, 1, 256, hs)
                    act(Kb[0][64:96, 0:256], bank[1][64:96, 0:256], AF.Identity, [bankT[1], tC], [], bias=cols[64:96, 3:4])
                    act(Kb[1][64:96, 0:256], bank[1][64:96, 0:256], AF.Identity, [bankT[1], tC], [], bias=cols[64:96, 3:4])
                    yield
                    rms_back([ckvn[:, 0:256]], 256, rq2, rq2T, 2)
                    yield
                    return
                tok0 = c * 512
                ko = CTX + tok0
                dma("sp", "rp0", rp[64:96, :, :], rope_v[:, :, tok0:tok0 + 512], writes=[rpT])
                proj(0, 0, 512, hs)
                proj(1, 1, 512, hs)
                rms_front([0, 1], 256.0, 13, 0, 20, 512, rq, rqT, 0)
                yield
                proj(2, 2, 512, hs)
                rms_front([2], 128.0, 15, 2, 22, 512, rq2, rq2T, 2)
                rms_back([qlatn[:, 0, tok0:tok0 + 512], qlatn[:, 1, tok0:tok0 + 512]], 512, rq, rqT, 0)
                yield
                proj(3, 3, 512, hs)
                proj(4, 6, 512, hs)
                rms_back([ckvn[:, ko:ko + 512]], 512, rq2, rq2T, 2)
                yield
                stt("dve", kt1[64:96, :], bank[3][64:96, :], cols[64:96, 3:4], rp[64:96, 0, :], ALU.add, ALU.mult,
                    [bankT[3], rpT, tC], [sqT[0]])
                stt("dve", kt2[64:96, :], bank[6][64:96, :], cols[64:96, 4:5], rp[64:96, 1, :], ALU.add, ALU.mult,
                    [bankT[6], rpT, tC], [sqT[1]])
                tt("dve", Kb[0][64:96, ko:ko + 512], kt1[64:96, :], kt2[64:96, :], ALU.add, [sqT[0], sqT[1]], [])
                tt("pool", Kb[1][64:96, ko:ko + 512], kt1[64:96, :], kt2[64:96, :], ALU.add, [sqT[0], sqT[1]], [])
                for j in range(4):
                    bk = nb([0, 1, 2, 3])
                    proj(5 + j, bk, 512, hs)
                    act(R2[:, j, tok0:tok0 + 512], bank[bk], AF.Silu, [bankT[bk], tC], [], bias=cols[:, 5 + j:6 + j])
                    yield

            load_ab(L, 2)
            tiles = []
            for ti in range(2):
                tiles.append(([(0, 128, ctx_d[b, ti * 128:(ti + 1) * 128, :])], 1, ti, -1 if ti == 1 else None, None))
            for c in range(8):
                for ti in range(4):
                    tok0 = c * 512
                    pre = (lambda: load_ab(L, b)) if (c == 0 and ti == 0) else None
                    tiles.append(([(0, 128, x_d[b, tok0 + ti * 128:tok0 + (ti + 1) * 128, :])], c % 2, ti, c if ti == 3 else None, pre))
            ln_pipeline(L, tiles, a2_chunk, 2)
            sc.barrier()

            ptr[0] = baseAt
            Vb = [alloc(34 * 65, BF16).rearrange("p (t v) -> p t v", v=65), alloc(34 * 128, BF16).rearrange("p (t v) -> p t v", v=128)]
            ropeT = alloc(S, F32)
            Wq = alloc(2 * 1024, BF16).rearrange("p (k n) -> p k n", k=2)
            Wkv = alloc(1024, BF16)
            Pt = [alloc(512, BF16) for _ in range(6)]
            PtT = [Tok() for _ in range(6)]
            Qt = [alloc(512, BF16) for _ in range(2)]
            QtT = [Tok(), Tok()]
            rd = alloc(512, F32)
            rdT = Tok()
            rdh = alloc(512, BF16)
            rdl = alloc(512, BF16)
            rdt = alloc(512, F32)
            onesb = alloc(128, BF16)
            cp("dve", onesb, ones, [tC], [tC])
            rb = alloc(512, F32)
            rbT = Tok()
            ot = alloc(512, F32)
            otT = Tok()
            tA = Tok()
            dma("sp", "ropeT", ropeT[64:96, :], rope_d[0], writes=[tA])
            dma("sp", "ropeT2", ropeT[96:128, :], rope_d[1], writes=[tA])
            wq_h = wq_bf.rearrange("(k p) (h c) -> p k h c", p=128, c=96)
            Wq_h = Wq.rearrange("p k (h c) -> p k h c", c=128)
            for k in range(2):
                dma("sp", "wq0", Wq_h[:, k, :, 0:96], wq_h[:, k, :, :], reads=[tWbf], writes=[tA])
                dma("sp", "wq1", Wq_h[:, k, :, 96:112], wq_h[:, k, :, 80:96], reads=[tWbf], writes=[tA])
                dma("sp", "wq2", Wq_h[:, k, :, 112:128], wq_h[:, k, :, 64:80], reads=[tWbf], writes=[tA])
            dma("sp", "wkv", Wkv, wkv_bf, reads=[tWbf], writes=[tA])
            tV = [Tok(), Tok()]
            ms("dve", Vb[0][:, :, 64:65], 1.0, [], [tV[0]])
            ms("dve", Vb[1][:, :, 0:64], 0.0, [], [tV[1]])
            ms("dve", Vb[1][:, :, 0:1], 1.0, [tV[1]], [tV[1]])
            tK = [Tok(), Tok()]
            for p_ in range(2):
                dma("sp", "kdup%d" % p_, Kb[p_][96:128, :], Kb[p_][64:96, :], writes=[tK[p_]])
            sgrp = [(0, 1), (2, 3)]

            def kv_pieces(h):
                par = h % 2
                K_, V_ = Kb[par], Vb[par]
                voff = 0 if par == 0 else 64
                pcs = []

                def kpiece(kc):
                    n = 512 if kc < 8 else 256
                    bk = nb([6, 7])
                    mm(bank[bk][0:64, 0:n], Wkv[:, h * 128:h * 128 + 64], ckvn[:, kc * 512:kc * 512 + n], True, True, [tA], [bankT[bk]])
                    cp("dve", K_[0:64, kc * 512:kc * 512 + n], bank[bk][0:64, 0:n], [bankT[bk]], [tK[par]])

                def vpiece(g0):
                    ng = min(8, 34 - g0)
                    bk = nb([6, 7])
                    for i in range(ng):
                        mm(bank[bk][:, i * 64:(i + 1) * 64], ckvn[:, (g0 + i) * 128:(g0 + i + 1) * 128], Wkv[:, h * 128 + 64:h * 128 + 128],
                           True, True, [tA], [bankT[bk]])
                    cp("dve", V_[:, g0:g0 + ng, voff:voff + 64], bank[bk][:, 0:ng * 64].rearrange("p (t v) -> p t v", v=64),
                       [bankT[bk]], [tV[par]])

                for kc in range(9):
                    pcs.append((lambda kc_: (lambda: kpiece(kc_)))(kc))
                for g0 in range(0, 34, 8):
                    pcs.append((lambda g_: (lambda: vpiece(g_)))(g0))
                return pcs

            def gen_q(h, qc, qs):
                q0 = qc * 512
                bq = nb([6, 7])
                for kk in range(2):
                    mm(bank[bq], Wq[:, kk, h * 128:(h + 1) * 128], qlatn[:, kk, q0:q0 + 512], kk == 0, kk == 1, [tA], [bankT[bq]])
                cp("dve", Qt[qs][0:64, :], bank[bq][0:64, :], [bankT[bq]], [QtT[qs]])
                tt("dve", Qt[qs][64:128, :], bank[bq][64:128, :], ropeT[64:128, q0:q0 + 512], ALU.mult, [bankT[bq], tA], [QtT[qs]])

            def post_a(h, qc, ob):
                dp = 64 if h % 2 == 0 else 0
                sc.add("dve", lambda e: e.reciprocal(rd[dp:dp + 1, :], bank[ob][dp:dp + 1, :]), [bankT[ob]], [rdT])
                cp("dve", rdh[dp:dp + 1, :], rd[dp:dp + 1, :], [rdT], [rdT])
                cp("dve", rdt[dp:dp + 1, :], rdh[dp:dp + 1, :], [rdT], [rdT])
                tt("dve", rdt[dp:dp + 1, :], rd[dp:dp + 1, :], rdt[dp:dp + 1, :], ALU.subtract, [rdT], [rdT])
                cp("dve", rdl[dp:dp + 1, :], rdt[dp:dp + 1, :], [rdT], [rdT])

            def post_b(h, qc, ob):
                par = h % 2
                q0 = qc * 512
                dp = 64 if par == 0 else 0
                o0, o1 = (0, 64) if par == 0 else (64, 128)
                bb = nb([6, 7])
                mm(bank[bb], onesb[dp:dp + 1, :], rdh[dp:dp + 1, :], True, False, [rdT, tC], [bankT[bb]])
                mm(bank[bb], onesb[dp:dp + 1, :], rdl[dp:dp + 1, :], False, True, [rdT, tC], [bankT[bb]])
                cp("dve", rb[o0:o1, :], bank[bb][o0:o1, :], [bankT[bb]], [rbT])
                tt("dve", ot[o0:o1, :], bank[ob][o0:o1, :], rb[o0:o1, :], ALU.mult, [bankT[ob], rbT], [otT])
                tt("pool", R2[o0:o1, h // 2, q0:q0 + 512], ot[o0:o1, :], R2[o0:o1, h // 2, q0:q0 + 512], ALU.mult, [otT], [])

            steps = []
            it = 0
            for h in range(H):
                for qc in range(8):
                    for kt in range(34):
                        steps.append((h, qc, kt, it))
                    it += 1
            NS = len(steps)
            sched_at = {}

            def at(g, fn):
                sched_at.setdefault(min(g, NS - 1), []).append(fn)

            for h in range(1, H):
                base = (h - 1) * 272 + 34
                for i, pc in enumerate(kv_pieces(h)):
                    at(base + 12 * i, pc)

            def s_pre(g):
                h, qc, kt, it_ = steps[g]
                if kt == 0:
                    gen_q(h, qc, it_ % 2)

            def s_qk(g):
                h, qc, kt, it_ = steps[g]
                sb = g % 4
                mm(bank[sb], Kb[h % 2][:, kt * 128:(kt + 1) * 128], Qt[it_ % 2], True, True,
                   [tK[h % 2], QtT[it_ % 2]], [bankT[sb]])

            def s_exp_pv(g):
                h, qc, kt, it_ = steps[g]
                sb = g % 4
                par = h % 2
                M = 65 if par == 0 else 128
                ob = 4 + (it_ % 2)
                pi = g % 6
                act(Pt[pi], bank[sb], AF.Exp, [bankT[sb]], [PtT[pi]], scale=SCALE)
                mm(bank[ob][0:M, :], Vb[par][:, kt, 0:M], Pt[pi], kt == 0, kt == 33, [tV[par], PtT[pi]], [bankT[ob]])
                if kt == 33:
                    post_a(h, qc, ob)
                    at(g + 20, (lambda h_, qc_, ob_: (lambda: post_b(h_, qc_, ob_)))(h, qc, ob))

            LOOK = 12
            SK = 3
            for pc in kv_pieces(0):
                pc()
            for g in range(min(LOOK, NS)):
                s_pre(g)
            for g in range(min(SK, NS)):
                s_qk(g)
            for g in range(NS):
                if g + LOOK < NS:
                    s_pre(g + LOOK)
                if g + SK < NS:
                    s_qk(g + SK)
                s_exp_pv(g)
                for fn in sched_at.pop(g, []):
                    fn()
            assert not sched_at
            sc.barrier()

            ptr[0] = base0
            Wo = alloc(8 * D, BF16).rearrange("p (k n) -> p k n", k=8)
            gate = alloc(D, F32)
            pgt = alloc(D, F32)
            pbt = alloc(D, F32)
            bo_f = alloc(D, F32)
            bo_h32 = alloc(D, F32)
            bo_hi = alloc(D, BF16)
            bo_lo = alloc(D, BF16)
            ones_bf = alloc(128, BF16)
            NSL = 4
            xt = [alloc(D, F32) for _ in range(NSL)]
            xtT = [Tok() for _ in range(NSL)]
            rr_ = [alloc(D, F32) for _ in range(NSL)]
            rrT = [Tok() for _ in range(NSL)]
            stl = [(alloc(12, F32), alloc(2, F32), alloc(1, F32), alloc(1, F32), Tok()) for _ in range(NSL)]
            tCc = Tok()
            dma("sp", "gate", gate[0:1, :], m_scr[b:b + 1, 2 * D:3 * D], reads=[tscr], writes=[tCc])
            dma("sp", "gb", bo_f[0:1, :], bo_d, writes=[tCc])
            tWo = Tok()
            dma("sp", "wo", Wo, wo_g[b].rearrange("(k p) n -> p k n", p=128), writes=[tWo])
            tt("dve", bo_f[0:1, :], bo_f[0:1, :], gate[0:1, :], ALU.mult, [tCc], [tCc])
            dma("sp", "pgt", pgt, pg_d.partition_broadcast(128), writes=[tCc])
            dma("sp", "pbt", pbt, pb_d.partition_broadcast(128), writes=[tCc])
            cp("dve", bo_hi[0:1, :], bo_f[0:1, :], [tCc], [tCc])
            cp("dve", bo_h32[0:1, :], bo_hi[0:1, :], [tCc], [tCc])
            tt("dve", bo_h32[0:1, :], bo_f[0:1, :], bo_h32[0:1, :], ALU.subtract, [tCc], [tCc])
            cp("dve", bo_lo[0:1, :], bo_h32[0:1, :], [tCc], [tCc])
            cp("dve", ones_bf[0:1, :], ones[0:1, :], [tC], [tCc])

            def c_a(ti):
                i = ti % NSL
                t0_ = ti * 128
                st_, mv_, rs_, nb_, stT = stl[i]
                dma("sp", "cx%d" % i, xt[i], x_d[b, t0_:t0_ + 128, :], writes=[xtT[i]])
                pg_ = [(0, 1), (2, 3), (4, 5), (6, 7)][ti % 4]
                for hf in range(2):
                    mm(bank[pg_[hf]], ones_bf[0:1, :], bo_hi[0:1, hf * 512:(hf + 1) * 512], True, False, [tCc], [bankT[pg_[hf]]])
                    mm(bank[pg_[hf]], ones_bf[0:1, :], bo_lo[0:1, hf * 512:(hf + 1) * 512], False, False, [tCc], [bankT[pg_[hf]]])
                    for k in range(8):
                        src = R2[:, k, t0_:t0_ + 128] if k < 4 else R1[:, k - 4, t0_:t0_ + 128]
                        mm(bank[pg_[hf]], src, Wo[:, k, hf * 512:(hf + 1) * 512], False, k == 7, [tCc, tWo], [bankT[pg_[hf]]])
                yb = psum[:, pg_[0] * 512:pg_[0] * 512 + 1024]
                r = rr_[i]
                stt("dve", r, xt[i], ALPHA, yb, ALU.mult, ALU.add, [bankT[pg_[0]], bankT[pg_[1]], xtT[i]], [rrT[i]])
                sc.add("dve", lambda e: e.bn_stats(st_[:, 0:6], r[:, 0:512]), [rrT[i]], [stT])
                sc.add("dve", lambda e: e.bn_stats(st_[:, 6:12], r[:, 512:1024]), [rrT[i]], [stT])
                sc.add("dve", lambda e: e.bn_aggr(mv_, st_), [stT], [stT])
                tt("pool", rs_, mv_[:, 1:2], epsc[:, 0:1], ALU.add, [stT, tC], [stT])
                tt("pool", rs_, rs_, mhalf[:, 0:1], ALU.pow, [stT, tC], [stT])

            def c_b1(ti):
                i = ti % NSL
                st_, mv_, rs_, nb_, stT = stl[i]
                r = rr_[i]
                ts("dve", nb_, mv_[:, 0:1], rs_[:, 0:1], -1.0, ALU.mult, ALU.mult, [stT], [stT])
                act(r, r, AF.Identity, [rrT[i], stT], [rrT[i]], bias=nb_[:, 0:1], scale=rs_[:, 0:1])

            def c_b2(ti):
                i = ti % NSL
                t0_ = ti * 128
                r = rr_[i]
                tt("dve", r, r, pgt, ALU.mult, [rrT[i], tCc], [rrT[i]])
                tt("pool", r, r, pbt, ALU.add, [rrT[i], tCc], [rrT[i]])
                dma("act", "out%d" % i, out_d[b, t0_:t0_ + 128, :], r, reads=[rrT[i]], writes=[rrT[i]])

            for ti in range(-2, 32):
                if 0 <= ti + 2 < 32:
                    c_a(ti + 2)
                if 0 <= ti + 1 < 32:
                    c_b1(ti + 1)
                if ti >= 0:
                    c_b2(ti)
            sc.barrier()
        sc.emit(nc, final_waits=[ch("out0"), ch("out1"), ch("out2"), ch("out3")])
    return nc


def _consts():
    c = {}
    c["ident"] = np.eye(128, dtype=np.float32).astype(NPBF)
    c["ones"] = np.ones((128, 128), np.float32)
    t1 = np.arange(64)[:, None].astype(np.float64)
    k1 = np.arange(64)[None, :].astype(np.float64)
    ang = 2 * np.pi * t1 * k1 / 64.0
    C64, S64 = np.cos(ang), np.sin(ang)
    cs = np.zeros((64, 2, 2, 32))
    for kh in range(2):
        cs[:, kh, 0, :] = C64[:, kh * 32:(kh + 1) * 32]
        cs[:, kh, 1, :] = S64[:, kh * 32:(kh + 1) * 32]
    c["cs64"] = cs.reshape(64, 128).astype(np.float32).astype(NPBF)
    cc = np.arange(128)[:, None].astype(np.float64)
    mm_ = np.arange(128)[None, :].astype(np.float64)
    a2 = 2 * np.pi * cc * mm_ / 128.0
    ffm = np.concatenate([np.cos(a2), -np.sin(a2)], axis=1) / math.sqrt(128.0)
    c["ff"] = ffm.astype(np.float32)
    t2 = np.arange(64).astype(np.float64)
    tab = np.zeros((8, 128, 8, 3, 64))
    for kc in range(8):
        for k1l in range(8):
            k = (kc * 8 + k1l) + 64 * np.arange(64).astype(np.float64)
            al = 2 * np.pi * t2[:, None] * k[None, :] / 4096.0
            blk = np.stack([-np.sin(al), np.cos(al), np.sin(al)], axis=1) / 64.0
            tab[kc, 0:64, k1l] = blk
            tab[kc, 64:128, k1l] = blk
    c["tab"] = tab.reshape(8, 128, 1536).astype(np.float32).astype(NPBF)
    t = np.arange(S)
    rows = (t // 64).astype(np.float32)
    colsg = (t % 64).astype(np.float32)
    inv = (10000.0 ** (-np.arange(0, 16, 2, dtype=np.float32) / 16.0)).astype(np.float32)
    ang = np.concatenate([rows[:, None] * inv, colsg[:, None] * inv], axis=-1)
    ang = np.concatenate([ang, ang], axis=-1)
    cosT = np.cos(ang).astype(np.float32).T
    sinT = np.sin(ang).astype(np.float32).T.copy()
    sinT[0:16] *= -1.0
    c["rope"] = np.ascontiguousarray(np.stack([cosT, sinT], axis=0)).astype(np.float32)
    return c


def _in_maps(inp, NB, ncores):
    f = lambda a: np.ascontiguousarray(np.asarray(a, dtype=np.float32))
    x, c, ctx, c_ctx = f(inp["x"]), f(inp["c"]), f(inp["ctx"]), f(inp["c_ctx"])
    b_in = f(inp["b_in"])[0]
    qg, kvg, b4 = f(inp["q_norm_g"])[0], f(inp["kv_norm_g"])[0], f(inp["b_fourier"])[0]
    cols = np.zeros((128, 32), np.float32)
    cols[:, 0] = b_in[0:128]
    cols[:, 1] = b_in[128:256]
    cols[:, 2] = b_in[256:384]
    cols[64:96, 3] = b_in[384:416]
    cols[64:80, 4] = b_in[400:416]
    cols[80:96, 4] = b_in[384:400]
    for j in range(4):
        cols[:, 5 + j] = b_in[416 + 128 * j:416 + 128 * (j + 1)]
        cols[:, 9 + j] = b_in[1440 + 128 * j:1440 + 128 * (j + 1)]
        cols[:, 16 + j] = b4[128 * j:128 * (j + 1)]
    cols[:, 13] = qg[0:128]
    cols[:, 14] = qg[128:256]
    cols[:, 15] = kvg
    k = _consts()
    shared = {
        "w_ada": f(inp["w_ada"])[0], "b_ada": f(inp["b_ada"]), "w_in": f(inp["w_in"])[0],
        "b_fin": np.ascontiguousarray(b_in[None, 928:1440]), "cols": cols,
        "w_q_up": f(inp["w_q_up"])[0], "w_kv_up": f(inp["w_kv_up"])[0], "w_fourier": f(inp["w_fourier"])[0],
        "w_out": f(inp["w_out"])[0], "b_out": f(inp["b_out"]), "post_ln_g": f(inp["post_ln_g"]),
        "post_ln_b": f(inp["post_ln_b"]),
    }
    shared.update(k)
    maps = []
    for i in range(ncores):
        cv = np.zeros((3, D), np.float32)
        cv[0:NB] = c[i * NB:(i + 1) * NB]
        cv[2] = c_ctx
        cT = np.ascontiguousarray(cv.reshape(3, 8, 128).transpose(2, 1, 0).reshape(128, 24))
        m = dict(shared)
        m["x"] = np.ascontiguousarray(x[i * NB:(i + 1) * NB])
        m["ctx"] = np.ascontiguousarray(ctx[i * NB:(i + 1) * NB])
        m["cT"] = cT
        maps.append(m)
    return maps


def kernel(**inputs):
    NB, ncores = 2, 8
    nc = build(NB)
    maps = _in_maps(inputs, NB, ncores)
    res = run_bass_kernel_spmd(nc, maps, core_ids=list(range(ncores)))
    return np.concatenate([np.asarray(r["out"], dtype=np.float32) for r in res.results], axis=0)
```

```python
import contextlib
import math

import ml_dtypes
import numpy as np

import concourse.bass as bass
import concourse.mybir as mybir
from concourse.bass_utils import run_bass_kernel_spmd

F32, BF16, U8 = mybir.dt.float32, mybir.dt.bfloat16, mybir.dt.uint8
AF = mybir.ActivationFunctionType
ALU = mybir.AluOpType
NPBF = ml_dtypes.bfloat16

D = 1024
S = 4096
CTX = 256
NK = S + CTX
H = 8
EPS = 1e-6
ALPHA = 2.0 ** 0.25
SCALE = 1.0 / math.sqrt(96.0)
ARENA = 191 * 1024 + 512

SAME_SYNC = {"act", "dve", "pool"}


class Tok:
    __slots__ = ("writer", "readers")

    def __init__(self):
        self.writer = None
        self.readers = []


class Op:
    __slots__ = ("eng", "fn", "deps", "signal", "count", "chan", "pos", "waits")

    def __init__(self, eng, fn, chan=None):
        self.eng, self.fn, self.chan = eng, fn, chan
        self.deps = []
        self.signal = False
        self.count = None


class Chan:
    def __init__(self, name):
        self.name, self.sem, self.n, self.last = name, None, 0, None


class Sched:
    ENGS = ("pe", "act", "dve", "pool", "sp")

    def __init__(self):
        self.ops = {e: [] for e in self.ENGS}
        self.chans = []
        self.pending = {e: [] for e in self.ENGS}

    def chan(self, name):
        c = Chan(name)
        self.chans.append(c)
        return c

    def add(self, eng, fn, reads=(), writes=(), chan=None):
        op = Op(eng, fn, chan)
        deps = list(self.pending[eng])
        self.pending[eng] = []
        for t in reads:
            if t.writer is not None:
                deps.append(t.writer)
        for t in writes:
            if t.writer is not None:
                deps.append(t.writer)
            deps.extend(t.readers)
        for t in reads:
            t.readers.append(op)
        for t in writes:
            t.writer = op
            t.readers = []
        seen = set()
        for d in deps:
            if d is op or id(d) in seen:
                continue
            seen.add(id(d))
            op.deps.append(d)
        if chan is not None:
            chan.last = op
        self.ops[eng].append(op)
        return op

    def barrier(self):
        lasts = [self.ops[e][-1] for e in self.ENGS if self.ops[e]]
        lasts += [c.last for c in self.chans if c.last is not None]
        for e in self.ENGS:
            self.pending[e] = list(lasts)

    def finalize(self):
        for e in self.ENGS:
            for i, op in enumerate(self.ops[e]):
                op.pos = i
        for e in self.ENGS:
            seen = {}
            for op in self.ops[e]:
                cand = [d for d in op.deps if d.chan is None and not (d.eng == e and e not in SAME_SYNC)]
                cand.sort(key=lambda d: -d.pos)
                op.waits = []
                for d in cand:
                    if seen.get(d.eng, -1) >= d.pos:
                        continue
                    seen[d.eng] = d.pos
                    d.signal = True
                    op.waits.append(d)
        for e in self.ENGS:
            c = 0
            for op in self.ops[e]:
                if op.chan is not None:
                    op.chan.n += 16
                    op.count = op.chan.n
                elif op.signal:
                    c += 1
                    op.count = c

    def emit(self, nc, final_waits=()):
        self.finalize()
        with contextlib.ExitStack() as st:
            esem = {e: st.enter_context(nc.semaphore("s_" + e)) for e in self.ENGS}
            for c in self.chans:
                c.sem = st.enter_context(nc.semaphore("c_" + c.name))
            blk = st.enter_context(nc.Block())

            def run(engname):
                def body(eng):
                    seen = {}
                    for op in self.ops[engname]:
                        for d in op.deps:
                            if d.chan is None:
                                continue
                            key, sem, val = ("c", id(d.chan)), d.chan.sem, d.count
                            if seen.get(key, 0) >= val:
                                continue
                            seen[key] = val
                            eng.wait_ge(sem, val)
                        for d in op.waits:
                            eng.wait_ge(esem[d.eng], d.count)
                        ins = op.fn(eng)
                        if op.chan is not None:
                            ins.then_inc(op.chan.sem, 16)
                        elif op.signal:
                            ins.then_inc(esem[engname], 1)
                    if engname == "sp":
                        for c in final_waits:
                            if c.n:
                                eng.wait_ge(c.sem, c.n)

                return body

            blk.tensor(run("pe"))
            blk.scalar(run("act"))
            blk.vector(run("dve"))
            blk.gpsimd(run("pool"))
            blk.sync(run("sp"))


def build(NB, dbg=False):
    nc = bass.Bass("TRN2", target_bir_lowering=False)

    def din(name, shape, dt=F32):
        return nc.dram_tensor(name, list(shape), dt, kind="ExternalInput").ap()

    x_d = din("x", [NB, S, D])
    ctx_d = din("ctx", [NB, CTX, D])
    cT_d = din("cT", [128, 24])
    w_ada_d = din("w_ada", [D, 3 * D])
    b_ada_d = din("b_ada", [1, 3 * D])
    w_in_d = din("w_in", [D, 1952])
    bfin_d = din("b_fin", [1, 512])
    cols_d = din("cols", [128, 32])
    wq_d = din("w_q_up", [256, 768])
    wkv_d = din("w_kv_up", [128, 1024])
    w4_d = din("w_fourier", [512, 512])
    wo_d = din("w_out", [D, D])
    bo_d = din("b_out", [1, D])
    pg_d = din("post_ln_g", [1, D])
    pb_d = din("post_ln_b", [1, D])
    ident_d = din("ident", [128, 128], BF16)
    ones_d = din("ones", [128, 128])
    cs64_d = din("cs64", [64, 128], BF16)
    ff_d = din("ff", [128, 256])
    tab_d = din("tab", [8, 128, 1536], BF16)
    rope_d = din("rope", [2, 32, S])
    out_d = nc.dram_tensor("out", [NB, S, D], F32, kind="ExternalOutput").ap()
    m_scr = nc.dram_tensor("m_scr", [3, 3 * D], F32).ap()
    w_in_bf = nc.dram_tensor("w_in_bf", [D, 1952], BF16).ap()
    wq_bf = nc.dram_tensor("wq_bf", [256, 768], BF16).ap()
    wo_g = nc.dram_tensor("wo_g", [NB, D, D], BF16).ap()
    wkv_bf = nc.dram_tensor("wkv_bf", [128, 1024], BF16).ap()

    sc = Sched()
    with contextlib.ExitStack() as st:
        arena = st.enter_context(nc.sbuf_tensor("arena", [128, ARENA], U8))
        psum = st.enter_context(nc.psum_tensor("psum", [128, 4096], F32))
        ptr = [0]

        def alloc(nel, dt):
            sz = nel * (4 if dt == F32 else 2)
            off = ptr[0]
            ptr[0] = off + (sz + 63) // 64 * 64
            assert ptr[0] <= ARENA, ("arena overflow", ptr[0])
            return arena[:, off:off + sz].bitcast(dt)

        bank = [psum[:, i * 512:(i + 1) * 512] for i in range(8)]
        bankT = [Tok() for _ in range(8)]
        _ch = {}

        def ch(name):
            if name not in _ch:
                _ch[name] = sc.chan(name)
            return _ch[name]

        def mm(out, lhsT, rhs, start, stop, reads, writes):
            return sc.add("pe", lambda e: e.matmul(out, lhsT, rhs, start=start, stop=stop), reads, writes)

        def tr(out, in_, reads, writes):
            return sc.add("pe", lambda e: e.transpose(out, in_, ident), reads, writes)

        def act(out, in_, func, reads, writes, bias=None, scale=None):
            kw = {}
            if bias is not None:
                kw["bias"] = bias
            if scale is not None:
                kw["scale"] = scale
            return sc.add("act", lambda e: e.activation(out, in_, func, **kw), reads, writes)

        def tt(eng, out, a, b, op, reads, writes):
            return sc.add(eng, lambda e: e.tensor_tensor(out, a, b, op), reads, writes)

        def ts(eng, out, a, s1, s2, op0, op1, reads, writes):
            if op1 is None:
                return sc.add(eng, lambda e: e.tensor_scalar(out, a, s1, None, op0), reads, writes)
            return sc.add(eng, lambda e: e.tensor_scalar(out, a, s1, s2, op0, op1), reads, writes)

        def stt(eng, out, a, s, b, op0, op1, reads, writes):
            return sc.add(eng, lambda e: e.scalar_tensor_tensor(out, a, s, b, op0, op1), reads, writes)

        def cp(eng, out, in_, reads, writes):
            if eng == "act":
                return sc.add(eng, lambda e: e.copy(out, in_), reads, writes)
            return sc.add(eng, lambda e: e.tensor_copy(out, in_), reads, writes)

        def ms(eng, out, val, reads, writes):
            return sc.add(eng, lambda e: e.memset(out, val), reads, writes)

        def dma(q, chname, out, in_, reads=(), writes=()):
            return sc.add(q, lambda e: e.dma_start(out=out, in_=in_), reads, writes, chan=ch(chname))

        ident = alloc(128, BF16)
        ones = alloc(128, F32)
        cs64 = alloc(128, BF16)
        ff = alloc(256, F32)
        cols = alloc(32, F32)
        bfin = alloc(512, F32)
        epsc = alloc(1, F32)
        mhalf = alloc(1, F32)
        R1 = alloc(4 * S, BF16).rearrange("p (j t) -> p j t", j=4)
        R2 = alloc(4 * S, BF16).rearrange("p (j t) -> p j t", j=4)
        tC = Tok()
        dma("sp", "c0", ident, ident_d, writes=[tC])
        dma("sp", "c1", ones, ones_d, writes=[tC])
        dma("sp", "c2", cs64[0:64, :], cs64_d, writes=[tC])
        dma("sp", "c3", ff, ff_d, writes=[tC])
        dma("sp", "c4", cols, cols_d, writes=[tC])
        dma("sp", "c5", bfin, bfin_d.partition_broadcast(128), writes=[tC])
        ms("dve", epsc, EPS, [], [tC])
        ms("dve", mhalf, -0.5, [], [tC])
        tt("dve", cols[:, 20:23], cols[:, 0:3], cols[:, 13:16], ALU.mult, [tC], [tC])
        base0 = ptr[0]
        tWbf = Tok()
        tWbfA = Tok()
        dma("pool", "cvt0a", w_in_bf[:, 928:1952], w_in_d[:, 928:1952], writes=[tWbfA])
        dma("pool", "cvt0", w_in_bf[:, 0:928], w_in_d[:, 0:928], writes=[tWbf])
        dma("pool", "cvt1", wq_bf, wq_d, writes=[tWbf])
        dma("pool", "cvt2", wkv_bf, wkv_d, writes=[tWbf])

        sct = alloc(24, F32)
        bada = alloc(3 * D, F32)
        mrow = alloc(3 * D, F32)
        wst = [alloc(8 * 512, F32).rearrange("p (k n) -> p k n", k=8) for _ in range(2)]
        wstT = [Tok(), Tok()]
        t0 = Tok()
        dma("sp", "p0a", sct, cT_d, writes=[t0])
        dma("sp", "p0b", bada[0:3, :], b_ada_d.partition_broadcast(3), writes=[t0])
        act(sct, sct, AF.Silu, [t0], [t0])
        wa_v = w_ada_d.rearrange("(k p) n -> p k n", p=128)
        tm = Tok()
        for n in range(6):
            sl = n % 2
            dma("sp" if sl == 0 else "act", "wst%d" % sl, wst[sl], wa_v[:, :, n * 512:(n + 1) * 512], writes=[wstT[sl]])
            for k in range(8):
                mm(bank[n % 2][0:3, :], sct[:, k * 3:(k + 1) * 3], wst[sl][:, k, :], k == 0, k == 7,
                   [t0, wstT[sl]], [bankT[n % 2]])
            tt("dve", mrow[0:3, n * 512:(n + 1) * 512], bank[n % 2][0:3, :], bada[0:3, n * 512:(n + 1) * 512],
               ALU.add, [bankT[n % 2], t0], [tm])
        tscr = Tok()
        dma("sp", "mscr", m_scr, mrow[0:3, :], reads=[tm], writes=[tscr])
        sc.barrier()

        def ln_common_alloc(nxt, nhb):
            d = {}
            d["nxt"], d["nhb"] = nxt, nhb
            d["A"] = alloc(D, F32)
            d["B"] = alloc(D, F32)
            d["xt"] = [alloc(D, F32) for _ in range(nxt)]
            d["hb"] = [alloc(D, BF16) for _ in range(nhb)]
            d["hT"] = [alloc(8 * 512, BF16).rearrange("p (k t) -> p k t", k=8) for _ in range(2)]
            d["st"] = [alloc(12, F32) for _ in range(nxt)]
            d["mv"] = [alloc(2, F32) for _ in range(nxt)]
            d["rs"] = [alloc(1, F32) for _ in range(nxt)]
            d["xtT"] = [Tok() for _ in range(nxt)]
            d["hbT"] = [Tok() for _ in range(nhb)]
            d["hTT"] = [Tok(), Tok()]
            d["stT"] = [Tok() for _ in range(nxt)]
            d["abT"] = Tok()
            return d

        def load_ab(L, row):
            dma("sp", "abA", L["A"], m_scr[row:row + 1, D:2 * D].partition_broadcast(128), reads=[tscr], writes=[L["abT"]])
            dma("sp", "abB", L["B"], m_scr[row:row + 1, 0:D].partition_broadcast(128), reads=[tscr], writes=[L["abT"]])
            ts("dve", L["A"], L["A"], 1.0, None, ALU.add, None, [L["abT"]], [L["abT"]])

        def ln_load(L, n, srcs):
            i = n % L["nxt"]
            for (p0, p1, ap) in srcs:
                dma("sp", "xt%d_%d" % (i, p0), L["xt"][i][p0:p1, :], ap, writes=[L["xtT"][i]])

        def ln_a(L, n, srcs):
            i = n % L["nxt"]
            xt, xtT, stT = L["xt"][i], L["xtT"][i], L["stT"][i]
            st_, mv_, rs_ = L["st"][i], L["mv"][i], L["rs"][i]
            sc.add("dve", lambda e: e.bn_stats(st_[:, 0:6], xt[:, 0:512]), [xtT], [stT])
            sc.add("dve", lambda e: e.bn_stats(st_[:, 6:12], xt[:, 512:1024]), [xtT], [stT])
            sc.add("dve", lambda e: e.bn_aggr(mv_, st_), [stT], [stT])
            tt("pool", rs_, mv_[:, 1:2], epsc[:, 0:1], ALU.add, [stT, tC], [stT])
            tt("pool", rs_, rs_, mhalf[:, 0:1], ALU.pow, [stT, tC], [stT])

        def ln_b(L, n, hs, ti):
            i = n % L["nxt"]
            xt, xtT, stT = L["xt"][i], L["xtT"][i], L["stT"][i]
            mv_, rs_ = L["mv"][i], L["rs"][i]
            stt("dve", xt, xt, mv_[:, 0:1], L["A"], ALU.subtract, ALU.mult, [xtT, stT, L["abT"]], [xtT])
            hb, hbT = L["hb"][n % L["nhb"]], L["hbT"][n % L["nhb"]]
            stt("dve", hb, xt, rs_[:, 0:1], L["B"], ALU.mult, ALU.add, [xtT, stT, L["abT"]], [hbT])
            pbk = 7
            pT = bank[pbk].bitcast(BF16).rearrange("p (k t) -> p k t", k=8)
            for k in range(8):
                tr(pT[:, k, :], hb[:, k * 128:(k + 1) * 128], [hbT, tC], [bankT[pbk]])
            cp("act", L["hT"][hs][:, :, ti * 128:(ti + 1) * 128], pT, [bankT[pbk]], [L["hTT"][hs]])

        def ln_pipeline(L, tiles, chunk_gen, per):
            pend = None
            ahead = L["nxt"] - 1
            for n in range(min(ahead, len(tiles))):
                ln_load(L, n, tiles[n][0])
            ln_a(L, 0, tiles[0][0])
            for n, (srcs, hs, ti, last, pre) in enumerate(tiles):
                if pre is not None:
                    pre()
                if n + ahead < len(tiles):
                    ln_load(L, n + ahead, tiles[n + ahead][0])
                if n + 1 < len(tiles):
                    ln_a(L, n + 1, tiles[n + 1][0])
                ln_b(L, n, hs, ti)
                if pend is not None:
                    for _ in range(per):
                        next(pend, None)
                if last is not None:
                    if pend is not None:
                        for _ in pend:
                            pass
                    pend = chunk_gen(last, hs)
            if pend is not None:
                for _ in pend:
                    pass

        w_in_v = w_in_bf.rearrange("(k p) n -> p k n", p=128)
        rr = [0]

        def nb(lst):
            rr[0] += 1
            return lst[rr[0] % len(lst)]

        for b in range(NB):
            ptr[0] = base0
            X = alloc(2 * 64 * 256, BF16)
            baseX = ptr[0]
            L = ln_common_alloc(3, 1)
            WA1 = alloc(8 * 1024, BF16).rearrange("p (k n) -> p k n", k=8)
            Xs = [alloc(512, BF16) for _ in range(2)]
            XsT = [Tok(), Tok()]
            tW = Tok()
            dma("sp", "wa1", WA1, w_in_v[:, :, 928:1952], reads=[tWbfA], writes=[tW])
            load_ab(L, b)
            xperm = x_d[b].rearrange("(t1 a j) d -> a j t1 d", a=32, j=2)
            Xv = X.rearrange("p (h t w) -> p h t w", h=2, t=64)

            def a1_chunk(c, hs):
                hT, hTT = L["hT"][hs], L["hTT"][hs]
                for ti in range(4):
                    t2 = 8 * c + 2 * ti
                    bkf = nb([0, 1, 2, 3, 4, 5])
                    for k in range(8):
                        mm(bank[bkf], hT[:, k, ti * 128:(ti + 1) * 128], WA1[:, k, 0:512], k == 0, k == 7, [tW, hTT], [bankT[bkf]])
                    yield
                    j = ti
                    bk = nb([0, 1, 2, 3, 4, 5])
                    for k in range(8):
                        mm(bank[bk], WA1[:, k, 512 + j * 128:512 + (j + 1) * 128], hT[:, k, :], k == 0, k == 7,
                           [tW, hTT], [bankT[bk]])
                    ov = R1[:, j, :].rearrange("p (t1 c ti j) -> p c ti j t1", c=8, ti=4, j=2)[:, c]
                    act(ov, bank[bk].rearrange("p (ti j t1) -> p ti j t1", ti=4, j=2), AF.Silu, [bankT[bk], tC], [],
                        bias=cols[:, 9 + j:10 + j])
                    xs_, xsT_ = Xs[(4 * c + ti) % 2], XsT[(4 * c + ti) % 2]
                    tt("dve", xs_, bank[bkf], bfin, ALU.add, [bankT[bkf], tC], [xsT_])
                    for jj in range(2):
                        dma("sp", "xs%d_%d" % ((4 * c + ti) % 2, jj), Xv[0:64, :, t2 + jj, :],
                            xs_[64 * jj:64 * jj + 64, :].rearrange("p (h w) -> p h w", h=2), reads=[xsT_], writes=[])
                    yield

            tiles = []
            for c in range(8):
                for ti in range(4):
                    a = 4 * c + ti
                    tiles.append(([(0, 64, xperm[a, 0]), (64, 128, xperm[a, 1])], c % 2, ti, c if ti == 3 else None, None))
            ln_pipeline(L, tiles, a1_chunk, 2)
            sc.barrier()

            ptr[0] = baseX
            W4s = alloc(4 * 512, F32).rearrange("p (g n) -> p g n", g=4)
            W4p = alloc(8 * 512, BF16).rearrange("p (j n) -> p j n", j=8)
            G = R2.rearrange("p j t -> p (j t)")
            GT = Tok()
            tabs = [alloc(1536, BF16) for _ in range(2)]
            tabT = [Tok(), Tok()]
            Ych = [alloc(8 * 512, BF16) for _ in range(2)]
            YchT = [Tok(), Tok()]
            gate_bc = alloc(D, F32)
            wof = [alloc(D, F32) for _ in range(2)]
            wog = [alloc(D, BF16) for _ in range(2)]
            wofT = [Tok(), Tok()]
            wogT = [Tok(), Tok()]
            tG = Tok()
            dma("sp", "gatebc", gate_bc, m_scr[b:b + 1, 2 * D:3 * D].partition_broadcast(128), reads=[tscr], writes=[tG])

            def fold_chunk(k):
                sl = k % 2
                dma("sp", "wof%d" % sl, wof[sl], wo_d[k * 128:(k + 1) * 128, :], writes=[wofT[sl]])
                tt("pool", wog[sl], wof[sl], gate_bc, ALU.mult, [wofT[sl], tG], [wogT[sl]])
                dma("pool", "wog%d" % sl, wo_g[b, k * 128:(k + 1) * 128, :], wog[sl], reads=[wogT[sl]], writes=[wogT[sl]])

            tW4 = Tok()
            dma("sp", "w4s", W4s, w4_d.rearrange("(g p) n -> p g n", p=128), writes=[tW4])
            tW4p = Tok()
            for g in range(4):
                for cs in range(2):
                    bk = nb([0, 1, 2, 3])
                    mm(bank[bk], ff[:, cs * 128:(cs + 1) * 128], W4s[:, g, :], True, True, [tW4, tC], [bankT[bk]])
                    cp("dve", W4p[:, g * 2 + cs, :], bank[bk], [bankT[bk]], [tW4p])
            for kh in range(2):
                for c0 in range(0, 256, 8):
                    bk = nb([0, 1, 2, 3])
                    for cw in range(c0, c0 + 8):
                        mm(bank[bk][:, (cw - c0) * 64:(cw - c0 + 1) * 64], X[0:64, cw:cw + 127 * 256 + 1:256],
                           cs64[0:64, kh * 64:(kh + 1) * 64], True, True, [tC], [bankT[bk]])
                    cp("act" if (c0 // 8) % 2 else "dve", G[:, c0 * 64:(c0 + 8) * 64], bank[bk], [bankT[bk]], [GT])
                for kc4 in range(4):
                    kc = kh * 4 + kc4
                    sl = kc % 2
                    dma("sp", "tab%d" % sl, tabs[sl], tab_d[kc], writes=[tabT[sl]])
                    fold_chunk(kc)
                    Yv = Ych[sl].rearrange("p (g c k q) -> p g c k q", g=4, c=2, k=8)
                    for g in range(4):
                        chf, gg = g // 2, g % 2
                        for half in range(2):
                            bk = nb([4, 5, 6, 7])
                            for k1i in range(4):
                                k1l = half * 4 + k1i
                                k1h = kc4 * 8 + k1l
                                for cs in range(2):
                                    o0 = gg * 128 * 64 + cs * 32 + k1h
                                    to = k1l * 192 + (64 if cs == 0 else 0)
                                    mm(bank[bk][:, k1i * 128:(k1i + 1) * 128], G[64 * chf:64 * chf + 64, o0:o0 + 127 * 64 + 1:64],
                                       tabs[sl][64 * chf:64 * chf + 64, to:to + 128], cs == 0, cs == 1, [GT, tabT[sl]], [bankT[bk]])
                            cp("act" if half else "dve", Yv[:, g, :, half * 4:(half + 1) * 4, :],
                               bank[bk].rearrange("p (k c q) -> p c k q", k=4, c=2), [bankT[bk]], [YchT[sl]])
                    Yf = Ych[sl].rearrange("p (j t) -> p j t", j=8)
                    for nt in range(4):
                        bk = nb([0, 1, 2, 3])
                        for j in range(8):
                            mm(bank[bk], W4p[:, j, nt * 128:(nt + 1) * 128], Yf[:, j, :], j == 0, j == 7, [tW4p, YchT[sl]], [bankT[bk]])
                        rv = R1[:, nt, :].rearrange("p (k2 k1) -> p k1 k2", k1=64)[:, kc * 8:(kc + 1) * 8, :]
                        stt("dve", rv, bank[bk].rearrange("p (k q) -> p k q", k=8), cols[:, 16 + nt:17 + nt], rv, ALU.add, ALU.mult,
                            [bankT[bk], tC], [])
            sc.barrier()

            ptr[0] = base0
            qlatn = alloc(2 * S, BF16).rearrange("p (j t) -> p j t", j=2)
            ckvn = alloc(NK, BF16)
            Kb = [alloc(NK, BF16) for _ in range(2)]
            baseAt = ptr[0]
            L = ln_common_alloc(3, 2)
            WA2 = alloc(8 * 1280, BF16).rearrange("p (k n) -> p k n", k=8)
            rp = alloc(2 * 512, F32).rearrange("p (i t) -> p i t", i=2)
            rpT = Tok()
            sq = [alloc(512, F32) for _ in range(2)]
            sqT = [Tok(), Tok()]
            rq = alloc(512, F32)
            rqT = Tok()
            qg = [alloc(512, F32) for _ in range(3)]
            qgT = [Tok() for _ in range(3)]
            rq2 = alloc(512, F32)
            rq2T = Tok()
            kt1, kt2 = sq[0], sq[1]
            tW = Tok()
            ms("pool", WA2[:, :, 384:640], 0.0, [], [tW])
            dma("sp", "wa2a", WA2[:, :, 0:384], w_in_v[:, :, 0:384], reads=[tWbf], writes=[tW])
            dma("sp", "wa2b", WA2[:, :, 640:1152], w_in_v[:, :, 416:928], reads=[tWbf], writes=[tW])
            dma("sp", "wa2c", WA2[:, :, 384 + 64:384 + 96], w_in_v[:, :, 384:416], reads=[tWbf], writes=[tW])
            dma("sp", "wa2d", WA2[:, :, 512 + 64:512 + 80], w_in_v[:, :, 400:416], reads=[tWbf], writes=[tW])
            dma("sp", "wa2e", WA2[:, :, 512 + 80:512 + 96], w_in_v[:, :, 384:400], reads=[tWbf], writes=[tW])
            rope_v = rope_d.rearrange("i p t -> p i t")

            def rms_front(pbs, nfeat, gcol, bcol, bgcol, n, rq_, rqT_, qgi):
                pss = nb([4, 5])
                for j, pb in enumerate(pbs):
                    act(sq[j][:, 0:n], bank[pb][:, 0:n], AF.Square, [bankT[pb], tC], [sqT[j]], bias=cols[:, bcol + j:bcol + j + 1])
                    mm(bank[pss][:, 0:n], ones, sq[j][:, 0:n], j == 0, j == len(pbs) - 1, [sqT[j], tC], [bankT[pss]])
                act(rq_[:, 0:n], bank[pss][:, 0:n], AF.Ln, [bankT[pss]], [rqT_], bias=epsc[:, 0:1], scale=1.0 / nfeat)
                act(rq_[:, 0:n], rq_[:, 0:n], AF.Exp, [rqT_], [rqT_], scale=-0.5)
                for j, pb in enumerate(pbs):
                    act(qg[qgi + j][:, 0:n], bank[pb][:, 0:n], AF.Identity, [bankT[pb], tC], [qgT[qgi + j]],
                        bias=cols[:, bgcol + j:bgcol + j + 1], scale=cols[:, gcol + j:gcol + j + 1])

            def rms_back(outs, n, rq_, rqT_, qgi):
                for j, o in enumerate(outs):
                    tt("dve", o, qg[qgi + j][:, 0:n], rq_[:, 0:n], ALU.mult, [qgT[qgi + j], rqT_], [])

            def proj(tile_idx, bk, n, hs):
                for k in range(8):
                    mm(bank[bk][:, 0:n], WA2[:, k, tile_idx * 128:(tile_idx + 1) * 128], L["hT"][hs][:, k, 0:n], k == 0, k == 7,
                       [tW, L["hTT"][hs]], [bankT[bk]])

            def a2_chunk(c, hs):
                if c < 0:
                    proj(2, 0, 256, hs)
                    rms_front([0], 128.0, 15, 2, 22, 256, rq2, rq2T, 2)
                    proj(3, 1, 256, hs)
                    act(Kb[0][64:96, 0:256], bank[1][64:96, 0:256], AF.Identity, [bankT[1], tC], [], bias=cols[64:96, 3:4])
                    act(Kb[1][64:96, 0:256], bank[1][64:96, 0:256], AF.Identity, [bankT[1], tC], [], bias=cols[64:96, 3:4])
                    yield
                    rms_back([ckvn[:, 0:256]], 256, rq2, rq2T, 2)
                    yield
                    return
                tok0 = c * 512
                ko = CTX + tok0
                dma("sp", "rp0", rp[64:96, :, :], rope_v[:, :, tok0:tok0 + 512], writes=[rpT])
                proj(0, 0, 512, hs)
                proj(1, 1, 512, hs)
                rms_front([0, 1], 256.0, 13, 0, 20, 512, rq, rqT, 0)
                yield
                proj(2, 2, 512, hs)
                rms_front([2], 128.0, 15, 2, 22, 512, rq2, rq2T, 2)
                rms_back([qlatn[:, 0, tok0:tok0 + 512], qlatn[:, 1, tok0:tok0 + 512]], 512, rq, rqT, 0)
                yield
                proj(3, 3, 512, hs)
                proj(4, 6, 512, hs)
                rms_back([ckvn[:, ko:ko + 512]], 512, rq2, rq2T, 2)
                yield
                stt("dve", kt1[64:96, :], bank[3][64:96, :], cols[64:96, 3:4], rp[64:96, 0, :], ALU.add, ALU.mult,
                    [bankT[3], rpT, tC], [sqT[0]])
                stt("dve", kt2[64:96, :], bank[6][64:96, :], cols[64:96, 4:5], rp[64:96, 1, :], ALU.add, ALU.mult,
                    [bankT[6], rpT, tC], [sqT[1]])
                tt("dve", Kb[0][64:96, ko:ko + 512], kt1[64:96, :], kt2[64:96, :], ALU.add, [sqT[0], sqT[1]], [])
                tt("pool", Kb[1][64:96, ko:ko + 512], kt1[64:96, :], kt2[64:96, :], ALU.add, [sqT[0], sqT[1]], [])
                for j in range(4):
                    bk = nb([0, 1, 2, 3])
                    proj(5 + j, bk, 512, hs)
                    act(R2[:, j, tok0:tok0 + 512], bank[bk], AF.Silu, [bankT[bk], tC], [], bias=cols[:, 5 + j:6 + j])
                    yield

            load_ab(L, 2)
            tiles = []
            for ti in range(2):
                tiles.append(([(0, 128, ctx_d[b, ti * 128:(ti + 1) * 128, :])], 1, ti, -1 if ti == 1 else None, None))
            for c in range(8):
                for ti in range(4):
                    tok0 = c * 512
                    pre = (lambda: load_ab(L, b)) if (c == 0 and ti == 0) else None
                    tiles.append(([(0, 128, x_d[b, tok0 + ti * 128:tok0 + (ti + 1) * 128, :])], c % 2, ti, c if ti == 3 else None, pre))
            ln_pipeline(L, tiles, a2_chunk, 2)
            sc.barrier()

            ptr[0] = baseAt
            Vb = [alloc(34 * 65, BF16).rearrange("p (t v) -> p t v", v=65), alloc(34 * 128, BF16).rearrange("p (t v) -> p t v", v=128)]
            ropeT = alloc(S, F32)
            Wq = alloc(2 * 1024, BF16).rearrange("p (k n) -> p k n", k=2)
            Wkv = alloc(1024, BF16)
            Pt = [alloc(512, BF16) for _ in range(6)]
            PtT = [Tok() for _ in range(6)]
            Qt = [alloc(512, BF16) for _ in range(2)]
            QtT = [Tok(), Tok()]
            rd = alloc(512, F32)
            rdT = Tok()
            rdh = alloc(512, BF16)
            rdl = alloc(512, BF16)
            rdt = alloc(512, F32)
            onesb = alloc(128, BF16)
            cp("dve", onesb, ones, [tC], [tC])
            rb = alloc(512, F32)
            rbT = Tok()
            ot = alloc(512, F32)
            otT = Tok()
            tA = Tok()
            dma("sp", "ropeT", ropeT[64:96, :], rope_d[0], writes=[tA])
            dma("sp", "ropeT2", ropeT[96:128, :], rope_d[1], writes=[tA])
            wq_h = wq_bf.rearrange("(k p) (h c) -> p k h c", p=128, c=96)
            Wq_h = Wq.rearrange("p k (h c) -> p k h c", c=128)
            for k in range(2):
                dma("sp", "wq0", Wq_h[:, k, :, 0:96], wq_h[:, k, :, :], reads=[tWbf], writes=[tA])
                dma("sp", "wq1", Wq_h[:, k, :, 96:112], wq_h[:, k, :, 80:96], reads=[tWbf], writes=[tA])
                dma("sp", "wq2", Wq_h[:, k, :, 112:128], wq_h[:, k, :, 64:80], reads=[tWbf], writes=[tA])
            dma("sp", "wkv", Wkv, wkv_bf, reads=[tWbf], writes=[tA])
            tV = [Tok(), Tok()]
            ms("dve", Vb[0][:, :, 64:65], 1.0, [], [tV[0]])
            ms("dve", Vb[1][:, :, 0:64], 0.0, [], [tV[1]])
            ms("dve", Vb[1][:, :, 0:1], 1.0, [tV[1]], [tV[1]])
            tK = [Tok(), Tok()]
            for p_ in range(2):
                dma("sp", "kdup%d" % p_, Kb[p_][96:128, :], Kb[p_][64:96, :], writes=[tK[p_]])
            sgrp = [(0, 1), (2, 3)]

            def kv_pieces(h):
                par = h % 2
                K_, V_ = Kb[par], Vb[par]
                voff = 0 if par == 0 else 64
                pcs = []

                def kpiece(kc):
                    n = 512 if kc < 8 else 256
                    bk = nb([6, 7])
                    mm(bank[bk][0:64, 0:n], Wkv[:, h * 128:h * 128 + 64], ckvn[:, kc * 512:kc * 512 + n], True, True, [tA], [bankT[bk]])
                    cp("dve", K_[0:64, kc * 512:kc * 512 + n], bank[bk][0:64, 0:n], [bankT[bk]], [tK[par]])

                def vpiece(g0):
                    ng = min(8, 34 - g0)
                    bk = nb([6, 7])
                    for i in range(ng):
                        mm(bank[bk][:, i * 64:(i + 1) * 64], ckvn[:, (g0 + i) * 128:(g0 + i + 1) * 128], Wkv[:, h * 128 + 64:h * 128 + 128],
                           True, True, [tA], [bankT[bk]])
                    cp("dve", V_[:, g0:g0 + ng, voff:voff + 64], bank[bk][:, 0:ng * 64].rearrange("p (t v) -> p t v", v=64),
                       [bankT[bk]], [tV[par]])

                for kc in range(9):
                    pcs.append((lambda kc_: (lambda: kpiece(kc_)))(kc))
                for g0 in range(0, 34, 8):
                    pcs.append((lambda g_: (lambda: vpiece(g_)))(g0))
                return pcs

            def gen_q(h, qc, qs):
                q0 = qc * 512
                bq = nb([6, 7])
                for kk in range(2):
                    mm(bank[bq], Wq[:, kk, h * 128:(h + 1) * 128], qlatn[:, kk, q0:q0 + 512], kk == 0, kk == 1, [tA], [bankT[bq]])
                cp("dve", Qt[qs][0:64, :], bank[bq][0:64, :], [bankT[bq]], [QtT[qs]])
                tt("dve", Qt[qs][64:128, :], bank[bq][64:128, :], ropeT[64:128, q0:q0 + 512], ALU.mult, [bankT[bq], tA], [QtT[qs]])

            def post_a(h, qc, ob):
                dp = 64 if h % 2 == 0 else 0
                sc.add("dve", lambda e: e.reciprocal(rd[dp:dp + 1, :], bank[ob][dp:dp + 1, :]), [bankT[ob]], [rdT])
                cp("dve", rdh[dp:dp + 1, :], rd[dp:dp + 1, :], [rdT], [rdT])
                cp("dve", rdt[dp:dp + 1, :], rdh[dp:dp + 1, :], [rdT], [rdT])
                tt("dve", rdt[dp:dp + 1, :], rd[dp:dp + 1, :], rdt[dp:dp + 1, :], ALU.subtract, [rdT], [rdT])
                cp("dve", rdl[dp:dp + 1, :], rdt[dp:dp + 1, :], [rdT], [rdT])

            def post_b(h, qc, ob):
                par = h % 2
                q0 = qc * 512
                dp = 64 if par == 0 else 0
                o0, o1 = (0, 64) if par == 0 else (64, 128)
                bb = nb([6, 7])
                mm(bank[bb], onesb[dp:dp + 1, :], rdh[dp:dp + 1, :], True, False, [rdT, tC], [bankT[bb]])
                mm(bank[bb], onesb[dp:dp + 1, :], rdl[dp:dp + 1, :], False, True, [rdT, tC], [bankT[bb]])
                cp("dve", rb[o0:o1, :], bank[bb][o0:o1, :], [bankT[bb]], [rbT])
                tt("dve", ot[o0:o1, :], bank[ob][o0:o1, :], rb[o0:o1, :], ALU.mult, [bankT[ob], rbT], [otT])
                tt("pool", R2[o0:o1, h // 2, q0:q0 + 512], ot[o0:o1, :], R2[o0:o1, h // 2, q0:q0 + 512], ALU.mult, [otT], [])

            steps = []
            it = 0
            for h in range(H):
                for qc in range(8):
                    for kt in range(34):
                        steps.append((h, qc, kt, it))
                    it += 1
            NS = len(steps)
            sched_at = {}

            def at(g, fn):
                sched_at.setdefault(min(g, NS - 1), []).append(fn)

            for h in range(1, H):
                base = (h - 1) * 272 + 34
                for i, pc in enumerate(kv_pieces(h)):
                    at(base + 12 * i, pc)

            def s_pre(g):
                h, qc, kt, it_ = steps[g]
                if kt == 0:
                    gen_q(h, qc, it_ % 2)

            def s_qk(g):
                h, qc, kt, it_ = steps[g]
                sb = g % 4
                mm(bank[sb], Kb[h % 2][:, kt * 128:(kt + 1) * 128], Qt[it_ % 2], True, True,
                   [tK[h % 2], QtT[it_ % 2]], [bankT[sb]])

            def s_exp_pv(g):
                h, qc, kt, it_ = steps[g]
                sb = g % 4
                par = h % 2
                M = 65 if par == 0 else 128
                ob = 4 + (it_ % 2)
                pi = g % 6
                act(Pt[pi], bank[sb], AF.Exp, [bankT[sb]], [PtT[pi]], scale=SCALE)
                mm(bank[ob][0:M, :], Vb[par][:, kt, 0:M], Pt[pi], kt == 0, kt == 33, [tV[par], PtT[pi]], [bankT[ob]])
                if kt == 33:
                    post_a(h, qc, ob)
                    at(g + 20, (lambda h_, qc_, ob_: (lambda: post_b(h_, qc_, ob_)))(h, qc, ob))

            LOOK = 12
            SK = 3
            for pc in kv_pieces(0):
                pc()
            for g in range(min(LOOK, NS)):
                s_pre(g)
            for g in range(min(SK, NS)):
                s_qk(g)
            for g in range(NS):
                if g + LOOK < NS:
                    s_pre(g + LOOK)
                if g + SK < NS:
                    s_qk(g + SK)
                s_exp_pv(g)
                for fn in sched_at.pop(g, []):
                    fn()
            assert not sched_at
            sc.barrier()

            ptr[0] = base0
            Wo = alloc(8 * D, BF16).rearrange("p (k n) -> p k n", k=8)
            gate = alloc(D, F32)
            pgt = alloc(D, F32)
            pbt = alloc(D, F32)
            bo_f = alloc(D, F32)
            bo_h32 = alloc(D, F32)
            bo_hi = alloc(D, BF16)
            bo_lo = alloc(D, BF16)
            ones_bf = alloc(128, BF16)
            NSL = 4
            xt = [alloc(D, F32) for _ in range(NSL)]
            xtT = [Tok() for _ in range(NSL)]
            rr_ = [alloc(D, F32) for _ in range(NSL)]
            rrT = [Tok() for _ in range(NSL)]
            stl = [(alloc(12, F32), alloc(2, F32), alloc(1, F32), alloc(1, F32), Tok()) for _ in range(NSL)]
            tCc = Tok()
            dma("sp", "gate", gate[0:1, :], m_scr[b:b + 1, 2 * D:3 * D], reads=[tscr], writes=[tCc])
            dma("sp", "gb", bo_f[0:1, :], bo_d, writes=[tCc])
            tWo = Tok()
            dma("sp", "wo", Wo, wo_g[b].rearrange("(k p) n -> p k n", p=128), writes=[tWo])
            tt("dve", bo_f[0:1, :], bo_f[0:1, :], gate[0:1, :], ALU.mult, [tCc], [tCc])
            dma("sp", "pgt", pgt, pg_d.partition_broadcast(128), writes=[tCc])
            dma("sp", "pbt", pbt, pb_d.partition_broadcast(128), writes=[tCc])
            cp("dve", bo_hi[0:1, :], bo_f[0:1, :], [tCc], [tCc])
            cp("dve", bo_h32[0:1, :], bo_hi[0:1, :], [tCc], [tCc])
            tt("dve", bo_h32[0:1, :], bo_f[0:1, :], bo_h32[0:1, :], ALU.subtract, [tCc], [tCc])
            cp("dve", bo_lo[0:1, :], bo_h32[0:1, :], [tCc], [tCc])
            cp("dve", ones_bf[0:1, :], ones[0:1, :], [tC], [tCc])

            def c_a(ti):
                i = ti % NSL
                t0_ = ti * 128
                st_, mv_, rs_, nb_, stT = stl[i]
                dma("sp", "cx%d" % i, xt[i], x_d[b, t0_:t0_ + 128, :], writes=[xtT[i]])
                pg_ = [(0, 1), (2, 3), (4, 5), (6, 7)][ti % 4]
                for hf in range(2):
                    mm(bank[pg_[hf]], ones_bf[0:1, :], bo_hi[0:1, hf * 512:(hf + 1) * 512], True, False, [tCc], [bankT[pg_[hf]]])
                    mm(bank[pg_[hf]], ones_bf[0:1, :], bo_lo[0:1, hf * 512:(hf + 1) * 512], False, False, [tCc], [bankT[pg_[hf]]])
                    for k in range(8):
                        src = R2[:, k, t0_:t0_ + 128] if k < 4 else R1[:, k - 4, t0_:t0_ + 128]
                        mm(bank[pg_[hf]], src, Wo[:, k, hf * 512:(hf + 1) * 512], False, k == 7, [tCc, tWo], [bankT[pg_[hf]]])
                yb = psum[:, pg_[0] * 512:pg_[0] * 512 + 1024]
                r = rr_[i]
                stt("dve", r, xt[i], ALPHA, yb, ALU.mult, ALU.add, [bankT[pg_[0]], bankT[pg_[1]], xtT[i]], [rrT[i]])
                sc.add("dve", lambda e: e.bn_stats(st_[:, 0:6], r[:, 0:512]), [rrT[i]], [stT])
                sc.add("dve", lambda e: e.bn_stats(st_[:, 6:12], r[:, 512:1024]), [rrT[i]], [stT])
                sc.add("dve", lambda e: e.bn_aggr(mv_, st_), [stT], [stT])
                tt("pool", rs_, mv_[:, 1:2], epsc[:, 0:1], ALU.add, [stT, tC], [stT])
                tt("pool", rs_, rs_, mhalf[:, 0:1], ALU.pow, [stT, tC], [stT])

            def c_b1(ti):
                i = ti % NSL
                st_, mv_, rs_, nb_, stT = stl[i]
                r = rr_[i]
                ts("dve", nb_, mv_[:, 0:1], rs_[:, 0:1], -1.0, ALU.mult, ALU.mult, [stT], [stT])
                act(r, r, AF.Identity, [rrT[i], stT], [rrT[i]], bias=nb_[:, 0:1], scale=rs_[:, 0:1])

            def c_b2(ti):
                i = ti % NSL
                t0_ = ti * 128
                r = rr_[i]
                tt("dve", r, r, pgt, ALU.mult, [rrT[i], tCc], [rrT[i]])
                tt("pool", r, r, pbt, ALU.add, [rrT[i], tCc], [rrT[i]])
                dma("act", "out%d" % i, out_d[b, t0_:t0_ + 128, :], r, reads=[rrT[i]], writes=[rrT[i]])

            for ti in range(-2, 32):
                if 0 <= ti + 2 < 32:
                    c_a(ti + 2)
                if 0 <= ti + 1 < 32:
                    c_b1(ti + 1)
                if ti >= 0:
                    c_b2(ti)
            sc.barrier()
        sc.emit(nc, final_waits=[ch("out0"), ch("out1"), ch("out2"), ch("out3")])
    return nc


def _consts():
    c = {}
    c["ident"] = np.eye(128, dtype=np.float32).astype(NPBF)
    c["ones"] = np.ones((128, 128), np.float32)
    t1 = np.arange(64)[:, None].astype(np.float64)
    k1 = np.arange(64)[None, :].astype(np.float64)
    ang = 2 * np.pi * t1 * k1 / 64.0
    C64, S64 = np.cos(ang), np.sin(ang)
    cs = np.zeros((64, 2, 2, 32))
    for kh in range(2):
        cs[:, kh, 0, :] = C64[:, kh * 32:(kh + 1) * 32]
        cs[:, kh, 1, :] = S64[:, kh * 32:(kh + 1) * 32]
    c["cs64"] = cs.reshape(64, 128).astype(np.float32).astype(NPBF)
    cc = np.arange(128)[:, None].astype(np.float64)
    mm_ = np.arange(128)[None, :].astype(np.float64)
    a2 = 2 * np.pi * cc * mm_ / 128.0
    ffm = np.concatenate([np.cos(a2), -np.sin(a2)], axis=1) / math.sqrt(128.0)
    c["ff"] = ffm.astype(np.float32)
    t2 = np.arange(64).astype(np.float64)
    tab = np.zeros((8, 128, 8, 3, 64))
    for kc in range(8):
        for k1l in range(8):
            k = (kc * 8 + k1l) + 64 * np.arange(64).astype(np.float64)
            al = 2 * np.pi * t2[:, None] * k[None, :] / 4096.0
            blk = np.stack([-np.sin(al), np.cos(al), np.sin(al)], axis=1) / 64.0
            tab[kc, 0:64, k1l] = blk
            tab[kc, 64:128, k1l] = blk
    c["tab"] = tab.reshape(8, 128, 1536).astype(np.float32).astype(NPBF)
    t = np.arange(S)
    rows = (t // 64).astype(np.float32)
    colsg = (t % 64).astype(np.float32)
    inv = (10000.0 ** (-np.arange(0, 16, 2, dtype=np.float32) / 16.0)).astype(np.float32)
    ang = np.concatenate([rows[:, None] * inv, colsg[:, None] * inv], axis=-1)
    ang = np.concatenate([ang, ang], axis=-1)
    cosT = np.cos(ang).astype(np.float32).T
    sinT = np.sin(ang).astype(np.float32).T.copy()
    sinT[0:16] *= -1.0
    c["rope"] = np.ascontiguousarray(np.stack([cosT, sinT], axis=0)).astype(np.float32)
    return c


def _in_maps(inp, NB, ncores):
    f = lambda a: np.ascontiguousarray(np.asarray(a, dtype=np.float32))
    x, c, ctx, c_ctx = f(inp["x"]), f(inp["c"]), f(inp["ctx"]), f(inp["c_ctx"])
    b_in = f(inp["b_in"])[0]
    qg, kvg, b4 = f(inp["q_norm_g"])[0], f(inp["kv_norm_g"])[0], f(inp["b_fourier"])[0]
    cols = np.zeros((128, 32), np.float32)
    cols[:, 0] = b_in[0:128]
    cols[:, 1] = b_in[128:256]
    cols[:, 2] = b_in[256:384]
    cols[64:96, 3] = b_in[384:416]
    cols[64:80, 4] = b_in[400:416]
    cols[80:96, 4] = b_in[384:400]
    for j in range(4):
        cols[:, 5 + j] = b_in[416 + 128 * j:416 + 128 * (j + 1)]
        cols[:, 9 + j] = b_in[1440 + 128 * j:1440 + 128 * (j + 1)]
        cols[:, 16 + j] = b4[128 * j:128 * (j + 1)]
    cols[:, 13] = qg[0:128]
    cols[:, 14] = qg[128:256]
    cols[:, 15] = kvg
    k = _consts()
    shared = {
        "w_ada": f(inp["w_ada"])[0], "b_ada": f(inp["b_ada"]), "w_in": f(inp["w_in"])[0],
        "b_fin": np.ascontiguousarray(b_in[None, 928:1440]), "cols": cols,
        "w_q_up": f(inp["w_q_up"])[0], "w_kv_up": f(inp["w_kv_up"])[0], "w_fourier": f(inp["w_fourier"])[0],
        "w_out": f(inp["w_out"])[0], "b_out": f(inp["b_out"]), "post_ln_g": f(inp["post_ln_g"]),
        "post_ln_b": f(inp["post_ln_b"]),
    }
    shared.update(k)
    maps = []
    for i in range(ncores):
        cv = np.zeros((3, D), np.float32)
        cv[0:NB] = c[i * NB:(i + 1) * NB]
        cv[2] = c_ctx
        cT = np.ascontiguousarray(cv.reshape(3, 8, 128).transpose(2, 1, 0).reshape(128, 24))
        m = dict(shared)
        m["x"] = np.ascontiguousarray(x[i * NB:(i + 1) * NB])
        m["ctx"] = np.ascontiguousarray(ctx[i * NB:(i + 1) * NB])
        m["cT"] = cT
        maps.append(m)
    return maps


def kernel(**inputs):
    NB, ncores = 2, 8
    nc = build(NB)
    maps = _in_maps(inputs, NB, ncores)
    res = run_bass_kernel_spmd(nc, maps, core_ids=list(range(ncores)))
    return np.concatenate([np.asarray(r["out"], dtype=np.float32) for r in res.results], axis=0)
```

```python
import contextlib
import math

import ml_dtypes
import numpy as np

import concourse.bass as bass
import concourse.mybir as mybir
from concourse.bass_utils import run_bass_kernel_spmd

F32, BF16, U8 = mybir.dt.float32, mybir.dt.bfloat16, mybir.dt.uint8
AF = mybir.ActivationFunctionType
ALU = mybir.AluOpType
NPBF = ml_dtypes.bfloat16

D = 1024
S = 4096
CTX = 256
NK = S + CTX
H = 8
EPS = 1e-6
ALPHA = 2.0 ** 0.25
SCALE = 1.0 / math.sqrt(96.0)
ARENA = 191 * 1024 + 512

SAME_SYNC = {"act", "dve", "pool"}


class Tok:
    __slots__ = ("writer", "readers")

    def __init__(self):
        self.writer = None
        self.readers = []


class Op:
    __slots__ = ("eng", "fn", "deps", "signal", "count", "chan", "pos", "waits")

    def __init__(self, eng, fn, chan=None):
        self.eng, self.fn, self.chan = eng, fn, chan
        self.deps = []
        self.signal = False
        self.count = None


class Chan:
    def __init__(self, name):
        self.name, self.sem, self.n, self.last = name, None, 0, None


class Sched:
    ENGS = ("pe", "act", "dve", "pool", "sp")

    def __init__(self):
        self.ops = {e: [] for e in self.ENGS}
        self.chans = []
        self.pending = {e: [] for e in self.ENGS}

    def chan(self, name):
        c = Chan(name)
        self.chans.append(c)
        return c

    def add(self, eng, fn, reads=(), writes=(), chan=None):
        op = Op(eng, fn, chan)
        deps = list(self.pending[eng])
        self.pending[eng] = []
        for t in reads:
            if t.writer is not None:
                deps.append(t.writer)
        for t in writes:
            if t.writer is not None:
                deps.append(t.writer)
            deps.extend(t.readers)
        for t in reads:
            t.readers.append(op)
        for t in writes:
            t.writer = op
            t.readers = []
        seen = set()
        for d in deps:
            if d is op or id(d) in seen:
                continue
            seen.add(id(d))
            op.deps.append(d)
        if chan is not None:
            chan.last = op
        self.ops[eng].append(op)
        return op

    def barrier(self):
        lasts = [self.ops[e][-1] for e in self.ENGS if self.ops[e]]
        lasts += [c.last for c in self.chans if c.last is not None]
        for e in self.ENGS:
            self.pending[e] = list(lasts)

    def finalize(self):
        for e in self.ENGS:
            for i, op in enumerate(self.ops[e]):
                op.pos = i
        for e in self.ENGS:
            seen = {}
            for op in self.ops[e]:
                cand = [d for d in op.deps if d.chan is None and not (d.eng == e and e not in SAME_SYNC)]
                cand.sort(key=lambda d: -d.pos)
                op.waits = []
                for d in cand:
                    if seen.get(d.eng, -1) >= d.pos:
                        continue
                    seen[d.eng] = d.pos
                    d.signal = True
                    op.waits.append(d)
        for e in self.ENGS:
            c = 0
            for op in self.ops[e]:
                if op.chan is not None:
                    op.chan.n += 16
                    op.count = op.chan.n
                elif op.signal:
                    c += 1
                    op.count = c

    def emit(self, nc, final_waits=()):
        self.finalize()
        with contextlib.ExitStack() as st:
            esem = {e: st.enter_context(nc.semaphore("s_" + e)) for e in self.ENGS}
            for c in self.chans:
                c.sem = st.enter_context(nc.semaphore("c_" + c.name))
            blk = st.enter_context(nc.Block())

            def run(engname):
                def body(eng):
                    seen = {}
                    for op in self.ops[engname]:
                        for d in op.deps:
                            if d.chan is None:
                                continue
                            key, sem, val = ("c", id(d.chan)), d.chan.sem, d.count
                            if seen.get(key, 0) >= val:
                                continue
                            seen[key] = val
                            eng.wait_ge(sem, val)
                        for d in op.waits:
                            eng.wait_ge(esem[d.eng], d.count)
                        ins = op.fn(eng)
                        if op.chan is not None:
                            ins.then_inc(op.chan.sem, 16)
                        elif op.signal:
                            ins.then_inc(esem[engname], 1)
                    if engname == "sp":
                        for c in final_waits:
                            if c.n:
                                eng.wait_ge(c.sem, c.n)

                return body

            blk.tensor(run("pe"))
            blk.scalar(run("act"))
            blk.vector(run("dve"))
            blk.gpsimd(run("pool"))
            blk.sync(run("sp"))


def build(NB, dbg=False):
    nc = bass.Bass("TRN2", target_bir_lowering=False)

    def din(name, shape, dt=F32):
        return nc.dram_tensor(name, list(shape), dt, kind="ExternalInput").ap()

    x_d = din("x", [NB, S, D])
    ctx_d = din("ctx", [NB, CTX, D])
    cT_d = din("cT", [128, 24])
    w_ada_d = din("w_ada", [D, 3 * D])
    b_ada_d = din("b_ada", [1, 3 * D])
    w_in_d = din("w_in", [D, 1952])
    bfin_d = din("b_fin", [1, 512])
    cols_d = din("cols", [128, 32])
    wq_d = din("w_q_up", [256, 768])
    wkv_d = din("w_kv_up", [128, 1024])
    w4_d = din("w_fourier", [512, 512])
    wo_d = din("w_out", [D, D])
    bo_d = din("b_out", [1, D])
    pg_d = din("post_ln_g", [1, D])
    pb_d = din("post_ln_b", [1, D])
    ident_d = din("ident", [128, 128], BF16)
    ones_d = din("ones", [128, 128])
    cs64_d = din("cs64", [64, 128], BF16)
    ff_d = din("ff", [128, 256])
    tab_d = din("tab", [8, 128, 1536], BF16)
    rope_d = din("rope", [2, 32, S])
    out_d = nc.dram_tensor("out", [NB, S, D], F32, kind="ExternalOutput").ap()
    m_scr = nc.dram_tensor("m_scr", [3, 3 * D], F32).ap()
    rd_scr = nc.dram_tensor("rd_scr", [2, 512], F32).ap()
    w_in_bf = nc.dram_tensor("w_in_bf", [D, 1952], BF16).ap()
    wq_bf = nc.dram_tensor("wq_bf", [256, 768], BF16).ap()
    wo_g = nc.dram_tensor("wo_g", [NB, D, D], BF16).ap()
    wkv_bf = nc.dram_tensor("wkv_bf", [128, 1024], BF16).ap()

    sc = Sched()
    with contextlib.ExitStack() as st:
        arena = st.enter_context(nc.sbuf_tensor("arena", [128, ARENA], U8))
        psum = st.enter_context(nc.psum_tensor("psum", [128, 4096], F32))
        ptr = [0]

        def alloc(nel, dt):
            sz = nel * (4 if dt == F32 else 2)
            off = ptr[0]
            ptr[0] = off + (sz + 63) // 64 * 64
            assert ptr[0] <= ARENA, ("arena overflow", ptr[0])
            return arena[:, off:off + sz].bitcast(dt)

        bank = [psum[:, i * 512:(i + 1) * 512] for i in range(8)]
        bankT = [Tok() for _ in range(8)]
        _ch = {}

        def ch(name):
            if name not in _ch:
                _ch[name] = sc.chan(name)
            return _ch[name]

        def mm(out, lhsT, rhs, start, stop, reads, writes):
            return sc.add("pe", lambda e: e.matmul(out, lhsT, rhs, start=start, stop=stop), reads, writes)

        def tr(out, in_, reads, writes):
            return sc.add("pe", lambda e: e.transpose(out, in_, ident), reads, writes)

        def act(out, in_, func, reads, writes, bias=None, scale=None):
            kw = {}
            if bias is not None:
                kw["bias"] = bias
            if scale is not None:
                kw["scale"] = scale
            return sc.add("act", lambda e: e.activation(out, in_, func, **kw), reads, writes)

        def tt(eng, out, a, b, op, reads, writes):
            return sc.add(eng, lambda e: e.tensor_tensor(out, a, b, op), reads, writes)

        def ts(eng, out, a, s1, s2, op0, op1, reads, writes):
            if op1 is None:
                return sc.add(eng, lambda e: e.tensor_scalar(out, a, s1, None, op0), reads, writes)
            return sc.add(eng, lambda e: e.tensor_scalar(out, a, s1, s2, op0, op1), reads, writes)

        def stt(eng, out, a, s, b, op0, op1, reads, writes):
            return sc.add(eng, lambda e: e.scalar_tensor_tensor(out, a, s, b, op0, op1), reads, writes)

        def cp(eng, out, in_, reads, writes):
            if eng == "act":
                return sc.add(eng, lambda e: e.copy(out, in_), reads, writes)
            return sc.add(eng, lambda e: e.tensor_copy(out, in_), reads, writes)

        def ms(eng, out, val, reads, writes):
            return sc.add(eng, lambda e: e.memset(out, val), reads, writes)

        def dma(q, chname, out, in_, reads=(), writes=()):
            return sc.add(q, lambda e: e.dma_start(out=out, in_=in_), reads, writes, chan=ch(chname))

        ident = alloc(128, BF16)
        ones = alloc(128, F32)
        cs64 = alloc(128, BF16)
        ff = alloc(256, F32)
        cols = alloc(32, F32)
        bfin = alloc(512, F32)
        epsc = alloc(1, F32)
        mhalf = alloc(1, F32)
        R1 = alloc(4 * S, BF16).rearrange("p (j t) -> p j t", j=4)
        R2 = alloc(4 * S, BF16).rearrange("p (j t) -> p j t", j=4)
        tC = Tok()
        dma("sp", "c0", ident, ident_d, writes=[tC])
        dma("sp", "c1", ones, ones_d, writes=[tC])
        dma("sp", "c2", cs64[0:64, :], cs64_d, writes=[tC])
        dma("sp", "c3", ff, ff_d, writes=[tC])
        dma("sp", "c4", cols, cols_d, writes=[tC])
        dma("sp", "c5", bfin, bfin_d.partition_broadcast(128), writes=[tC])
        ms("dve", epsc, EPS, [], [tC])
        ms("dve", mhalf, -0.5, [], [tC])
        tt("dve", cols[:, 20:23], cols[:, 0:3], cols[:, 13:16], ALU.mult, [tC], [tC])
        base0 = ptr[0]
        tWbf = Tok()
        tWbfA = Tok()
        dma("pool", "cvt0a", w_in_bf[:, 928:1952], w_in_d[:, 928:1952], writes=[tWbfA])
        dma("pool", "cvt0", w_in_bf[:, 0:928], w_in_d[:, 0:928], writes=[tWbf])
        dma("pool", "cvt1", wq_bf, wq_d, writes=[tWbf])
        dma("pool", "cvt2", wkv_bf, wkv_d, writes=[tWbf])

        sct = alloc(24, F32)
        bada = alloc(3 * D, F32)
        mrow = alloc(3 * D, F32)
        wst = [alloc(8 * 512, F32).rearrange("p (k n) -> p k n", k=8) for _ in range(2)]
        wstT = [Tok(), Tok()]
        t0 = Tok()
        dma("sp", "p0a", sct, cT_d, writes=[t0])
        dma("sp", "p0b", bada[0:3, :], b_ada_d.partition_broadcast(3), writes=[t0])
        act(sct, sct, AF.Silu, [t0], [t0])
        wa_v = w_ada_d.rearrange("(k p) n -> p k n", p=128)
        tm = Tok()
        for n in range(6):
            sl = n % 2
            dma("sp" if sl == 0 else "act", "wst%d" % sl, wst[sl], wa_v[:, :, n * 512:(n + 1) * 512], writes=[wstT[sl]])
            for k in range(8):
                mm(bank[n % 2][0:3, :], sct[:, k * 3:(k + 1) * 3], wst[sl][:, k, :], k == 0, k == 7,
                   [t0, wstT[sl]], [bankT[n % 2]])
            tt("dve", mrow[0:3, n * 512:(n + 1) * 512], bank[n % 2][0:3, :], bada[0:3, n * 512:(n + 1) * 512],
               ALU.add, [bankT[n % 2], t0], [tm])
        tscr = Tok()
        dma("sp", "mscr", m_scr, mrow[0:3, :], reads=[tm], writes=[tscr])
        sc.barrier()

        def ln_common_alloc(nxt, nhb):
            d = {}
            d["nxt"], d["nhb"] = nxt, nhb
            d["A"] = alloc(D, F32)
            d["B"] = alloc(D, F32)
            d["xt"] = [alloc(D, F32) for _ in range(nxt)]
            d["hb"] = [alloc(D, BF16) for _ in range(nhb)]
            d["hT"] = [alloc(8 * 512, BF16).rearrange("p (k t) -> p k t", k=8) for _ in range(2)]
            d["st"] = [alloc(12, F32) for _ in range(nxt)]
            d["mv"] = [alloc(2, F32) for _ in range(nxt)]
            d["rs"] = [alloc(1, F32) for _ in range(nxt)]
            d["xtT"] = [Tok() for _ in range(nxt)]
            d["hbT"] = [Tok() for _ in range(nhb)]
            d["hTT"] = [Tok(), Tok()]
            d["stT"] = [Tok() for _ in range(nxt)]
            d["abT"] = Tok()
            return d

        def load_ab(L, row):
            dma("sp", "abA", L["A"], m_scr[row:row + 1, D:2 * D].partition_broadcast(128), reads=[tscr], writes=[L["abT"]])
            dma("sp", "abB", L["B"], m_scr[row:row + 1, 0:D].partition_broadcast(128), reads=[tscr], writes=[L["abT"]])
            ts("dve", L["A"], L["A"], 1.0, None, ALU.add, None, [L["abT"]], [L["abT"]])

        def ln_load(L, n, srcs):
            i = n % L["nxt"]
            for (p0, p1, ap) in srcs:
                dma("sp", "xt%d_%d" % (i, p0), L["xt"][i][p0:p1, :], ap, writes=[L["xtT"][i]])

        def ln_a(L, n, srcs):
            i = n % L["nxt"]
            xt, xtT, stT = L["xt"][i], L["xtT"][i], L["stT"][i]
            st_, mv_, rs_ = L["st"][i], L["mv"][i], L["rs"][i]
            sc.add("dve", lambda e: e.bn_stats(st_[:, 0:6], xt[:, 0:512]), [xtT], [stT])
            sc.add("dve", lambda e: e.bn_stats(st_[:, 6:12], xt[:, 512:1024]), [xtT], [stT])
            sc.add("dve", lambda e: e.bn_aggr(mv_, st_), [stT], [stT])
            tt("pool", rs_, mv_[:, 1:2], epsc[:, 0:1], ALU.add, [stT, tC], [stT])
            tt("pool", rs_, rs_, mhalf[:, 0:1], ALU.pow, [stT, tC], [stT])

        def ln_b(L, n, hs, ti):
            i = n % L["nxt"]
            xt, xtT, stT = L["xt"][i], L["xtT"][i], L["stT"][i]
            mv_, rs_ = L["mv"][i], L["rs"][i]
            stt("dve", xt, xt, mv_[:, 0:1], L["A"], ALU.subtract, ALU.mult, [xtT, stT, L["abT"]], [xtT])
            hb, hbT = L["hb"][n % L["nhb"]], L["hbT"][n % L["nhb"]]
            stt("dve", hb, xt, rs_[:, 0:1], L["B"], ALU.mult, ALU.add, [xtT, stT, L["abT"]], [hbT])
            pbk = 7
            pT = bank[pbk].bitcast(BF16).rearrange("p (k t) -> p k t", k=8)
            for k in range(8):
                tr(pT[:, k, :], hb[:, k * 128:(k + 1) * 128], [hbT, tC], [bankT[pbk]])
            cp("act", L["hT"][hs][:, :, ti * 128:(ti + 1) * 128], pT, [bankT[pbk]], [L["hTT"][hs]])

        def ln_pipeline(L, tiles, chunk_gen, per):
            pend = None
            ahead = L["nxt"] - 1
            for n in range(min(ahead, len(tiles))):
                ln_load(L, n, tiles[n][0])
            ln_a(L, 0, tiles[0][0])
            for n, (srcs, hs, ti, last, pre) in enumerate(tiles):
                if pre is not None:
                    pre()
                if n + ahead < len(tiles):
                    ln_load(L, n + ahead, tiles[n + ahead][0])
                if n + 1 < len(tiles):
                    ln_a(L, n + 1, tiles[n + 1][0])
                ln_b(L, n, hs, ti)
                if pend is not None:
                    for _ in range(per):
                        next(pend, None)
                if last is not None:
                    if pend is not None:
                        for _ in pend:
                            pass
                    pend = chunk_gen(last, hs)
            if pend is not None:
                for _ in pend:
                    pass

        w_in_v = w_in_bf.rearrange("(k p) n -> p k n", p=128)
        rr = [0]

        def nb(lst):
            rr[0] += 1
            return lst[rr[0] % len(lst)]

        for b in range(NB):
            ptr[0] = base0
            X = alloc(2 * 64 * 256, BF16)
            baseX = ptr[0]
            L = ln_common_alloc(3, 1)
            WA1 = alloc(8 * 1024, BF16).rearrange("p (k n) -> p k n", k=8)
            Xs = [alloc(512, BF16) for _ in range(2)]
            XsT = [Tok(), Tok()]
            tW = Tok()
            dma("sp", "wa1", WA1, w_in_v[:, :, 928:1952], reads=[tWbfA], writes=[tW])
            load_ab(L, b)
            xperm = x_d[b].rearrange("(t1 a j) d -> a j t1 d", a=32, j=2)
            Xv = X.rearrange("p (h t w) -> p h t w", h=2, t=64)

            def a1_chunk(c, hs):
                hT, hTT = L["hT"][hs], L["hTT"][hs]
                for ti in range(4):
                    t2 = 8 * c + 2 * ti
                    bkf = nb([0, 1, 2, 3, 4, 5])
                    for k in range(8):
                        mm(bank[bkf], hT[:, k, ti * 128:(ti + 1) * 128], WA1[:, k, 0:512], k == 0, k == 7, [tW, hTT], [bankT[bkf]])
                    yield
                    j = ti
                    bk = nb([0, 1, 2, 3, 4, 5])
                    for k in range(8):
                        mm(bank[bk], WA1[:, k, 512 + j * 128:512 + (j + 1) * 128], hT[:, k, :], k == 0, k == 7,
                           [tW, hTT], [bankT[bk]])
                    ov = R1[:, j, :].rearrange("p (t1 c ti j) -> p c ti j t1", c=8, ti=4, j=2)[:, c]
                    act(ov, bank[bk].rearrange("p (ti j t1) -> p ti j t1", ti=4, j=2), AF.Silu, [bankT[bk], tC], [],
                        bias=cols[:, 9 + j:10 + j])
                    xs_, xsT_ = Xs[(4 * c + ti) % 2], XsT[(4 * c + ti) % 2]
                    tt("dve", xs_, bank[bkf], bfin, ALU.add, [bankT[bkf], tC], [xsT_])
                    for jj in range(2):
                        dma("sp", "xs%d_%d" % ((4 * c + ti) % 2, jj), Xv[0:64, :, t2 + jj, :],
                            xs_[64 * jj:64 * jj + 64, :].rearrange("p (h w) -> p h w", h=2), reads=[xsT_], writes=[])
                    yield

            tiles = []
            for c in range(8):
                for ti in range(4):
                    a = 4 * c + ti
                    tiles.append(([(0, 64, xperm[a, 0]), (64, 128, xperm[a, 1])], c % 2, ti, c if ti == 3 else None, None))
            ln_pipeline(L, tiles, a1_chunk, 2)
            sc.barrier()

            ptr[0] = baseX
            W4s = alloc(4 * 512, F32).rearrange("p (g n) -> p g n", g=4)
            W4p = alloc(8 * 512, BF16).rearrange("p (j n) -> p j n", j=8)
            G = R2.rearrange("p j t -> p (j t)")
            GT = Tok()
            tabs = [alloc(1536, BF16) for _ in range(2)]
            tabT = [Tok(), Tok()]
            Ych = [alloc(8 * 512, BF16) for _ in range(2)]
            YchT = [Tok(), Tok()]
            gate_bc = alloc(D, F32)
            wof = [alloc(D, F32) for _ in range(2)]
            wog = [alloc(D, BF16) for _ in range(2)]
            wofT = [Tok(), Tok()]
            wogT = [Tok(), Tok()]
            tG = Tok()
            dma("sp", "gatebc", gate_bc, m_scr[b:b + 1, 2 * D:3 * D].partition_broadcast(128), reads=[tscr], writes=[tG])

            def fold_chunk(k):
                sl = k % 2
                dma("sp", "wof%d" % sl, wof[sl], wo_d[k * 128:(k + 1) * 128, :], writes=[wofT[sl]])
                tt("pool", wog[sl], wof[sl], gate_bc, ALU.mult, [wofT[sl], tG], [wogT[sl]])
                dma("pool", "wog%d" % sl, wo_g[b, k * 128:(k + 1) * 128, :], wog[sl], reads=[wogT[sl]], writes=[wogT[sl]])

            tW4 = Tok()
            dma("sp", "w4s", W4s, w4_d.rearrange("(g p) n -> p g n", p=128), writes=[tW4])
            tW4p = Tok()
            for g in range(4):
                for cs in range(2):
                    bk = nb([0, 1, 2, 3])
                    mm(bank[bk], ff[:, cs * 128:(cs + 1) * 128], W4s[:, g, :], True, True, [tW4, tC], [bankT[bk]])
                    cp("dve", W4p[:, g * 2 + cs, :], bank[bk], [bankT[bk]], [tW4p])
            for kh in range(2):
                for c0 in range(0, 256, 8):
                    bk = nb([0, 1, 2, 3])
                    for cw in range(c0, c0 + 8):
                        mm(bank[bk][:, (cw - c0) * 64:(cw - c0 + 1) * 64], X[0:64, cw:cw + 127 * 256 + 1:256],
                           cs64[0:64, kh * 64:(kh + 1) * 64], True, True, [tC], [bankT[bk]])
                    cp("act" if (c0 // 8) % 2 else "dve", G[:, c0 * 64:(c0 + 8) * 64], bank[bk], [bankT[bk]], [GT])
                for kc4 in range(4):
                    kc = kh * 4 + kc4
                    sl = kc % 2
                    dma("sp", "tab%d" % sl, tabs[sl], tab_d[kc], writes=[tabT[sl]])
                    fold_chunk(kc)
                    Yv = Ych[sl].rearrange("p (g c k q) -> p g c k q", g=4, c=2, k=8)
                    for g in range(4):
                        chf, gg = g // 2, g % 2
                        for half in range(2):
                            bk = nb([4, 5, 6, 7])
                            for k1i in range(4):
                                k1l = half * 4 + k1i
                                k1h = kc4 * 8 + k1l
                                for cs in range(2):
                                    o0 = gg * 128 * 64 + cs * 32 + k1h
                                    to = k1l * 192 + (64 if cs == 0 else 0)
                                    mm(bank[bk][:, k1i * 128:(k1i + 1) * 128], G[64 * chf:64 * chf + 64, o0:o0 + 127 * 64 + 1:64],
                                       tabs[sl][64 * chf:64 * chf + 64, to:to + 128], cs == 0, cs == 1, [GT, tabT[sl]], [bankT[bk]])
                            cp("act" if half else "dve", Yv[:, g, :, half * 4:(half + 1) * 4, :],
                               bank[bk].rearrange("p (k c q) -> p c k q", k=4, c=2), [bankT[bk]], [YchT[sl]])
                    Yf = Ych[sl].rearrange("p (j t) -> p j t", j=8)
                    for nt in range(4):
                        bk = nb([0, 1, 2, 3])
                        for j in range(8):
                            mm(bank[bk], W4p[:, j, nt * 128:(nt + 1) * 128], Yf[:, j, :], j == 0, j == 7, [tW4p, YchT[sl]], [bankT[bk]])
                        rv = R1[:, nt, :].rearrange("p (k2 k1) -> p k1 k2", k1=64)[:, kc * 8:(kc + 1) * 8, :]
                        stt("dve", rv, bank[bk].rearrange("p (k q) -> p k q", k=8), cols[:, 16 + nt:17 + nt], rv, ALU.add, ALU.mult,
                            [bankT[bk], tC], [])
            sc.barrier()

            ptr[0] = base0
            qlatn = alloc(2 * S, BF16).rearrange("p (j t) -> p j t", j=2)
            ckvn = alloc(NK, BF16)
            Kb = [alloc(NK, BF16) for _ in range(2)]
            baseAt = ptr[0]
            L = ln_common_alloc(3, 2)
            WA2 = alloc(8 * 1280, BF16).rearrange("p (k n) -> p k n", k=8)
            rp = alloc(2 * 512, F32).rearrange("p (i t) -> p i t", i=2)
            rpT = Tok()
            sq = [alloc(512, F32) for _ in range(2)]
            sqT = [Tok(), Tok()]
            rq = alloc(512, F32)
            rqT = Tok()
            qg = [alloc(512, F32) for _ in range(3)]
            qgT = [Tok() for _ in range(3)]
            rq2 = alloc(512, F32)
            rq2T = Tok()
            kt1, kt2 = sq[0], sq[1]
            tW = Tok()
            ms("pool", WA2[:, :, 384:640], 0.0, [], [tW])
            dma("sp", "wa2a", WA2[:, :, 0:384], w_in_v[:, :, 0:384], reads=[tWbf], writes=[tW])
            dma("sp", "wa2b", WA2[:, :, 640:1152], w_in_v[:, :, 416:928], reads=[tWbf], writes=[tW])
            dma("sp", "wa2c", WA2[:, :, 384 + 64:384 + 96], w_in_v[:, :, 384:416], reads=[tWbf], writes=[tW])
            dma("sp", "wa2d", WA2[:, :, 512 + 64:512 + 80], w_in_v[:, :, 400:416], reads=[tWbf], writes=[tW])
            dma("sp", "wa2e", WA2[:, :, 512 + 80:512 + 96], w_in_v[:, :, 384:400], reads=[tWbf], writes=[tW])
            rope_v = rope_d.rearrange("i p t -> p i t")

            def rms_front(pbs, nfeat, gcol, bcol, bgcol, n, rq_, rqT_, qgi):
                pss = nb([4, 5])
                for j, pb in enumerate(pbs):
                    act(sq[j][:, 0:n], bank[pb][:, 0:n], AF.Square, [bankT[pb], tC], [sqT[j]], bias=cols[:, bcol + j:bcol + j + 1])
                    mm(bank[pss][:, 0:n], ones, sq[j][:, 0:n], j == 0, j == len(pbs) - 1, [sqT[j], tC], [bankT[pss]])
                act(rq_[:, 0:n], bank[pss][:, 0:n], AF.Ln, [bankT[pss]], [rqT_], bias=epsc[:, 0:1], scale=1.0 / nfeat)
                act(rq_[:, 0:n], rq_[:, 0:n], AF.Exp, [rqT_], [rqT_], scale=-0.5)
                for j, pb in enumerate(pbs):
                    act(qg[qgi + j][:, 0:n], bank[pb][:, 0:n], AF.Identity, [bankT[pb], tC], [qgT[qgi + j]],
                        bias=cols[:, bgcol + j:bgcol + j + 1], scale=cols[:, gcol + j:gcol + j + 1])

            def rms_back(outs, n, rq_, rqT_, qgi):
                for j, o in enumerate(outs):
                    tt("dve", o, qg[qgi + j][:, 0:n], rq_[:, 0:n], ALU.mult, [qgT[qgi + j], rqT_], [])

            def proj(tile_idx, bk, n, hs):
                for k in range(8):
                    mm(bank[bk][:, 0:n], WA2[:, k, tile_idx * 128:(tile_idx + 1) * 128], L["hT"][hs][:, k, 0:n], k == 0, k == 7,
                       [tW, L["hTT"][hs]], [bankT[bk]])

            def a2_chunk(c, hs):
                if c < 0:
                    proj(2, 0, 256, hs)
                    rms_front([0], 128.0, 15, 2, 22, 256, rq2, rq2T, 2)
                    proj(3, 1, 256, hs)
                    act(Kb[0][64:96, 0:256], bank[1][64:96, 0:256], AF.Identity, [bankT[1], tC], [], bias=cols[64:96, 3:4])
                    act(Kb[1][64:96, 0:256], bank[1][64:96, 0:256], AF.Identity, [bankT[1], tC], [], bias=cols[64:96, 3:4])
                    yield
                    rms_back([ckvn[:, 0:256]], 256, rq2, rq2T, 2)
                    yield
                    return
                tok0 = c * 512
                ko = CTX + tok0
                dma("sp", "rp0", rp[64:96, :, :], rope_v[:, :, tok0:tok0 + 512], writes=[rpT])
                proj(0, 0, 512, hs)
                proj(1, 1, 512, hs)
                rms_front([0, 1], 256.0, 13, 0, 20, 512, rq, rqT, 0)
                yield
                proj(2, 2, 512, hs)
                rms_front([2], 128.0, 15, 2, 22, 512, rq2, rq2T, 2)
                rms_back([qlatn[:, 0, tok0:tok0 + 512], qlatn[:, 1, tok0:tok0 + 512]], 512, rq, rqT, 0)
                yield
                proj(3, 3, 512, hs)
                proj(4, 6, 512, hs)
                rms_back([ckvn[:, ko:ko + 512]], 512, rq2, rq2T, 2)
                yield
                stt("dve", kt1[64:96, :], bank[3][64:96, :], cols[64:96, 3:4], rp[64:96, 0, :], ALU.add, ALU.mult,
                    [bankT[3], rpT, tC], [sqT[0]])
                stt("dve", kt2[64:96, :], bank[6][64:96, :], cols[64:96, 4:5], rp[64:96, 1, :], ALU.add, ALU.mult,
                    [bankT[6], rpT, tC], [sqT[1]])
                tt("dve", Kb[0][64:96, ko:ko + 512], kt1[64:96, :], kt2[64:96, :], ALU.add, [sqT[0], sqT[1]], [])
                tt("pool", Kb[1][64:96, ko:ko + 512], kt1[64:96, :], kt2[64:96, :], ALU.add, [sqT[0], sqT[1]], [])
                for j in range(4):
                    bk = nb([0, 1, 2, 3])
                    proj(5 + j, bk, 512, hs)
                    act(R2[:, j, tok0:tok0 + 512], bank[bk], AF.Silu, [bankT[bk], tC], [], bias=cols[:, 5 + j:6 + j])
                    yield

            load_ab(L, 2)
            tiles = []
            for ti in range(2):
                tiles.append(([(0, 128, ctx_d[b, ti * 128:(ti + 1) * 128, :])], 1, ti, -1 if ti == 1 else None, None))
            for c in range(8):
                for ti in range(4):
                    tok0 = c * 512
                    pre = (lambda: load_ab(L, b)) if (c == 0 and ti == 0) else None
                    tiles.append(([(0, 128, x_d[b, tok0 + ti * 128:tok0 + (ti + 1) * 128, :])], c % 2, ti, c if ti == 3 else None, pre))
            ln_pipeline(L, tiles, a2_chunk, 2)
            sc.barrier()

            ptr[0] = baseAt
            Vb = [alloc(34 * 65, BF16).rearrange("p (t v) -> p t v", v=65), alloc(34 * 128, BF16).rearrange("p (t v) -> p t v", v=128)]
            ropeT = alloc(S, F32)
            Wq = alloc(2 * 1024, BF16).rearrange("p (k n) -> p k n", k=2)
            Wkv = alloc(1024, BF16)
            Pt = [alloc(512, BF16) for _ in range(6)]
            PtT = [Tok() for _ in range(6)]
            Qt = [alloc(512, BF16) for _ in range(2)]
            QtT = [Tok(), Tok()]
            rd = alloc(512, F32)
            rdT = Tok()
            scrT = [Tok(), Tok()]
            rdh = alloc(512, BF16)
            rdl = alloc(512, BF16)
            rdt = alloc(512, F32)
            onesb = alloc(128, BF16)
            cp("dve", onesb, ones, [tC], [tC])
            rb = alloc(512, F32)
            rbT = Tok()
            ot = alloc(512, F32)
            otT = Tok()
            tA = Tok()
            dma("sp", "ropeT", ropeT[64:96, :], rope_d[0], writes=[tA])
            dma("sp", "ropeT2", ropeT[96:128, :], rope_d[1], writes=[tA])
            wq_h = wq_bf.rearrange("(k p) (h c) -> p k h c", p=128, c=96)
            Wq_h = Wq.rearrange("p k (h c) -> p k h c", c=128)
            for k in range(2):
                dma("sp", "wq0", Wq_h[:, k, :, 0:96], wq_h[:, k, :, :], reads=[tWbf], writes=[tA])
                dma("sp", "wq1", Wq_h[:, k, :, 96:112], wq_h[:, k, :, 80:96], reads=[tWbf], writes=[tA])
                dma("sp", "wq2", Wq_h[:, k, :, 112:128], wq_h[:, k, :, 64:80], reads=[tWbf], writes=[tA])
            dma("sp", "wkv", Wkv, wkv_bf, reads=[tWbf], writes=[tA])
            tV = [Tok(), Tok()]
            ms("dve", Vb[0][:, :, 64:65], 1.0, [], [tV[0]])
            ms("dve", Vb[1][:, :, 0:64], 0.0, [], [tV[1]])
            ms("dve", Vb[1][:, :, 0:1], 1.0, [tV[1]], [tV[1]])
            tK = [Tok(), Tok()]
            for p_ in range(2):
                dma("sp", "kdup%d" % p_, Kb[p_][96:128, :], Kb[p_][64:96, :], writes=[tK[p_]])
            sgrp = [(0, 1), (2, 3)]

            def kv_pieces(h):
                par = h % 2
                K_, V_ = Kb[par], Vb[par]
                voff = 0 if par == 0 else 64
                pcs = []

                def kpiece(kc):
                    n = 512 if kc < 8 else 256
                    bk = nb([6, 7])
                    mm(bank[bk][0:64, 0:n], Wkv[:, h * 128:h * 128 + 64], ckvn[:, kc * 512:kc * 512 + n], True, True, [tA], [bankT[bk]])
                    cp("dve", K_[0:64, kc * 512:kc * 512 + n], bank[bk][0:64, 0:n], [bankT[bk]], [tK[par]])

                def vpiece(g0):
                    ng = min(8, 34 - g0)
                    bk = nb([6, 7])
                    for i in range(ng):
                        mm(bank[bk][:, i * 64:(i + 1) * 64], ckvn[:, (g0 + i) * 128:(g0 + i + 1) * 128], Wkv[:, h * 128 + 64:h * 128 + 128],
                           True, True, [tA], [bankT[bk]])
                    cp("dve", V_[:, g0:g0 + ng, voff:voff + 64], bank[bk][:, 0:ng * 64].rearrange("p (t v) -> p t v", v=64),
                       [bankT[bk]], [tV[par]])

                for kc in range(9):
                    pcs.append((lambda kc_: (lambda: kpiece(kc_)))(kc))
                for g0 in range(0, 34, 8):
                    pcs.append((lambda g_: (lambda: vpiece(g_)))(g0))
                return pcs

            def gen_q(h, qc, qs):
                q0 = qc * 512
                bq = nb([6, 7])
                for kk in range(2):
                    mm(bank[bq], Wq[:, kk, h * 128:(h + 1) * 128], qlatn[:, kk, q0:q0 + 512], kk == 0, kk == 1, [tA], [bankT[bq]])
                cp("dve", Qt[qs][0:64, :], bank[bq][0:64, :], [bankT[bq]], [QtT[qs]])
                tt("dve", Qt[qs][64:128, :], bank[bq][64:128, :], ropeT[64:128, q0:q0 + 512], ALU.mult, [bankT[bq], tA], [QtT[qs]])

            def post_a(h, qc, ob):
                dp = 64 if h % 2 == 0 else 0
                sc.add("dve", lambda e: e.reciprocal(rd[dp:dp + 1, :], bank[ob][dp:dp + 1, :]), [bankT[ob]], [rdT])
                dma("sp", "rds%d" % (ob - 4), rd_scr[ob - 4:ob - 3, :], rd[dp:dp + 1, :], reads=[rdT], writes=[scrT[ob - 4]])

            def post_b(h, qc, ob):
                par = h % 2
                q0 = qc * 512
                dp = 64 if par == 0 else 0
                o0, o1 = (0, 64) if par == 0 else (64, 128)
                dma("sp", "rbc", rb[o0:o1, :], rd_scr[ob - 4:ob - 3, :].partition_broadcast(64), reads=[scrT[ob - 4]], writes=[rbT])
                tt("dve", ot[o0:o1, :], bank[ob][o0:o1, :], rb[o0:o1, :], ALU.mult, [bankT[ob], rbT], [otT])
                tt("pool", R2[o0:o1, h // 2, q0:q0 + 512], ot[o0:o1, :], R2[o0:o1, h // 2, q0:q0 + 512], ALU.mult, [otT], [])

            steps = []
            it = 0
            for h in range(H):
                for qc in range(8):
                    for kt in range(34):
                        steps.append((h, qc, kt, it))
                    it += 1
            NS = len(steps)
            sched_at = {}

            def at(g, fn):
                sched_at.setdefault(min(g, NS - 1), []).append(fn)

            for h in range(1, H):
                base = (h - 1) * 272 + 34
                for i, pc in enumerate(kv_pieces(h)):
                    at(base + 12 * i, pc)

            def s_pre(g):
                h, qc, kt, it_ = steps[g]
                if kt == 0:
                    gen_q(h, qc, it_ % 2)

            def s_qk(g):
                h, qc, kt, it_ = steps[g]
                sb = g % 4
                mm(bank[sb], Kb[h % 2][:, kt * 128:(kt + 1) * 128], Qt[it_ % 2], True, True,
                   [tK[h % 2], QtT[it_ % 2]], [bankT[sb]])

            def s_exp_pv(g):
                h, qc, kt, it_ = steps[g]
                sb = g % 4
                par = h % 2
                M = 65 if par == 0 else 128
                ob = 4 + (it_ % 2)
                pi = g % 6
                act(Pt[pi], bank[sb], AF.Exp, [bankT[sb]], [PtT[pi]], scale=SCALE)
                mm(bank[ob][0:M, :], Vb[par][:, kt, 0:M], Pt[pi], kt == 0, kt == 33, [tV[par], PtT[pi]], [bankT[ob]])
                if kt == 33:
                    post_a(h, qc, ob)
                    at(g + 20, (lambda h_, qc_, ob_: (lambda: post_b(h_, qc_, ob_)))(h, qc, ob))

            LOOK = 12
            SK = 3
            for pc in kv_pieces(0):
                pc()
            for g in range(min(LOOK, NS)):
                s_pre(g)
            for g in range(min(SK, NS)):
                s_qk(g)
            for g in range(NS):
                if g + LOOK < NS:
                    s_pre(g + LOOK)
                if g + SK < NS:
                    s_qk(g + SK)
                s_exp_pv(g)
                for fn in sched_at.pop(g, []):
                    fn()
            assert not sched_at
            sc.barrier()

            ptr[0] = base0
            Wo = alloc(8 * D, BF16).rearrange("p (k n) -> p k n", k=8)
            gate = alloc(D, F32)
            pgt = alloc(D, F32)
            pbt = alloc(D, F32)
            bo_f = alloc(D, F32)
            bo_h32 = alloc(D, F32)
            bo_hi = alloc(D, BF16)
            bo_lo = alloc(D, BF16)
            ones_bf = alloc(128, BF16)
            NSL = 4
            xt = [alloc(D, F32) for _ in range(NSL)]
            xtT = [Tok() for _ in range(NSL)]
            rr_ = [alloc(D, F32) for _ in range(NSL)]
            rrT = [Tok() for _ in range(NSL)]
            stl = [(alloc(12, F32), alloc(2, F32), alloc(1, F32), alloc(1, F32), Tok()) for _ in range(NSL)]
            tCc = Tok()
            dma("sp", "gate", gate[0:1, :], m_scr[b:b + 1, 2 * D:3 * D], reads=[tscr], writes=[tCc])
            dma("sp", "gb", bo_f[0:1, :], bo_d, writes=[tCc])
            tWo = Tok()
            dma("sp", "wo", Wo, wo_g[b].rearrange("(k p) n -> p k n", p=128), writes=[tWo])
            tt("dve", bo_f[0:1, :], bo_f[0:1, :], gate[0:1, :], ALU.mult, [tCc], [tCc])
            dma("sp", "pgt", pgt, pg_d.partition_broadcast(128), writes=[tCc])
            dma("sp", "pbt", pbt, pb_d.partition_broadcast(128), writes=[tCc])
            cp("dve", bo_hi[0:1, :], bo_f[0:1, :], [tCc], [tCc])
            cp("dve", bo_h32[0:1, :], bo_hi[0:1, :], [tCc], [tCc])
            tt("dve", bo_h32[0:1, :], bo_f[0:1, :], bo_h32[0:1, :], ALU.subtract, [tCc], [tCc])
            cp("dve", bo_lo[0:1, :], bo_h32[0:1, :], [tCc], [tCc])
            cp("dve", ones_bf[0:1, :], ones[0:1, :], [tC], [tCc])

            def c_a(ti):
                i = ti % NSL
                t0_ = ti * 128
                st_, mv_, rs_, nb_, stT = stl[i]
                dma("sp", "cx%d" % i, xt[i], x_d[b, t0_:t0_ + 128, :], writes=[xtT[i]])
                pg_ = [(0, 1), (2, 3), (4, 5), (6, 7)][ti % 4]
                for hf in range(2):
                    mm(bank[pg_[hf]], ones_bf[0:1, :], bo_hi[0:1, hf * 512:(hf + 1) * 512], True, False, [tCc], [bankT[pg_[hf]]])
                    mm(bank[pg_[hf]], ones_bf[0:1, :], bo_lo[0:1, hf * 512:(hf + 1) * 512], False, False, [tCc], [bankT[pg_[hf]]])
                    for k in range(8):
                        src = R2[:, k, t0_:t0_ + 128] if k < 4 else R1[:, k - 4, t0_:t0_ + 128]
                        mm(bank[pg_[hf]], src, Wo[:, k, hf * 512:(hf + 1) * 512], False, k == 7, [tCc, tWo], [bankT[pg_[hf]]])
                yb = psum[:, pg_[0] * 512:pg_[0] * 512 + 1024]
                r = rr_[i]
                stt("dve", r, xt[i], ALPHA, yb, ALU.mult, ALU.add, [bankT[pg_[0]], bankT[pg_[1]], xtT[i]], [rrT[i]])
                sc.add("dve", lambda e: e.bn_stats(st_[:, 0:6], r[:, 0:512]), [rrT[i]], [stT])
                sc.add("dve", lambda e: e.bn_stats(st_[:, 6:12], r[:, 512:1024]), [rrT[i]], [stT])
                sc.add("dve", lambda e: e.bn_aggr(mv_, st_), [stT], [stT])
                tt("pool", rs_, mv_[:, 1:2], epsc[:, 0:1], ALU.add, [stT, tC], [stT])
                tt("pool", rs_, rs_, mhalf[:, 0:1], ALU.pow, [stT, tC], [stT])

            def c_b1(ti):
                i = ti % NSL
                st_, mv_, rs_, nb_, stT = stl[i]
                r = rr_[i]
                ts("dve", nb_, mv_[:, 0:1], rs_[:, 0:1], -1.0, ALU.mult, ALU.mult, [stT], [stT])
                act(r, r, AF.Identity, [rrT[i], stT], [rrT[i]], bias=nb_[:, 0:1], scale=rs_[:, 0:1])

            def c_b2(ti):
                i = ti % NSL
                t0_ = ti * 128
                r = rr_[i]
                tt("dve", r, r, pgt, ALU.mult, [rrT[i], tCc], [rrT[i]])
                tt("pool", r, r, pbt, ALU.add, [rrT[i], tCc], [rrT[i]])
                dma("act", "out%d" % i, out_d[b, t0_:t0_ + 128, :], r, reads=[rrT[i]], writes=[rrT[i]])

            for ti in range(-2, 32):
                if 0 <= ti + 2 < 32:
                    c_a(ti + 2)
                if 0 <= ti + 1 < 32:
                    c_b1(ti + 1)
                if ti >= 0:
                    c_b2(ti)
            sc.barrier()
        sc.emit(nc, final_waits=[ch("out0"), ch("out1"), ch("out2"), ch("out3")])
    return nc


def _consts():
    c = {}
    c["ident"] = np.eye(128, dtype=np.float32).astype(NPBF)
    c["ones"] = np.ones((128, 128), np.float32)
    t1 = np.arange(64)[:, None].astype(np.float64)
    k1 = np.arange(64)[None, :].astype(np.float64)
    ang = 2 * np.pi * t1 * k1 / 64.0
    C64, S64 = np.cos(ang), np.sin(ang)
    cs = np.zeros((64, 2, 2, 32))
    for kh in range(2):
        cs[:, kh, 0, :] = C64[:, kh * 32:(kh + 1) * 32]
        cs[:, kh, 1, :] = S64[:, kh * 32:(kh + 1) * 32]
    c["cs64"] = cs.reshape(64, 128).astype(np.float32).astype(NPBF)
    cc = np.arange(128)[:, None].astype(np.float64)
    mm_ = np.arange(128)[None, :].astype(np.float64)
    a2 = 2 * np.pi * cc * mm_ / 128.0
    ffm = np.concatenate([np.cos(a2), -np.sin(a2)], axis=1) / math.sqrt(128.0)
    c["ff"] = ffm.astype(np.float32)
    t2 = np.arange(64).astype(np.float64)
    tab = np.zeros((8, 128, 8, 3, 64))
    for kc in range(8):
        for k1l in range(8):
            k = (kc * 8 + k1l) + 64 * np.arange(64).astype(np.float64)
            al = 2 * np.pi * t2[:, None] * k[None, :] / 4096.0
            blk = np.stack([-np.sin(al), np.cos(al), np.sin(al)], axis=1) / 64.0
            tab[kc, 0:64, k1l] = blk
            tab[kc, 64:128, k1l] = blk
    c["tab"] = tab.reshape(8, 128, 1536).astype(np.float32).astype(NPBF)
    t = np.arange(S)
    rows = (t // 64).astype(np.float32)
    colsg = (t % 64).astype(np.float32)
    inv = (10000.0 ** (-np.arange(0, 16, 2, dtype=np.float32) / 16.0)).astype(np.float32)
    ang = np.concatenate([rows[:, None] * inv, colsg[:, None] * inv], axis=-1)
    ang = np.concatenate([ang, ang], axis=-1)
    cosT = np.cos(ang).astype(np.float32).T
    sinT = np.sin(ang).astype(np.float32).T.copy()
    sinT[0:16] *= -1.0
    c["rope"] = np.ascontiguousarray(np.stack([cosT, sinT], axis=0)).astype(np.float32)
    return c


def _in_maps(inp, NB, ncores):
    f = lambda a: np.ascontiguousarray(np.asarray(a, dtype=np.float32))
    x, c, ctx, c_ctx = f(inp["x"]), f(inp["c"]), f(inp["ctx"]), f(inp["c_ctx"])
    b_in = f(inp["b_in"])[0]
    qg, kvg, b4 = f(inp["q_norm_g"])[0], f(inp["kv_norm_g"])[0], f(inp["b_fourier"])[0]
    cols = np.zeros((128, 32), np.float32)
    cols[:, 0] = b_in[0:128]
    cols[:, 1] = b_in[128:256]
    cols[:, 2] = b_in[256:384]
    cols[64:96, 3] = b_in[384:416]
    cols[64:80, 4] = b_in[400:416]
    cols[80:96, 4] = b_in[384:400]
    for j in range(4):
        cols[:, 5 + j] = b_in[416 + 128 * j:416 + 128 * (j + 1)]
        cols[:, 9 + j] = b_in[1440 + 128 * j:1440 + 128 * (j + 1)]
        cols[:, 16 + j] = b4[128 * j:128 * (j + 1)]
    cols[:, 13] = qg[0:128]
    cols[:, 14] = qg[128:256]
    cols[:, 15] = kvg
    k = _consts()
    shared = {
        "w_ada": f(inp["w_ada"])[0], "b_ada": f(inp["b_ada"]), "w_in": f(inp["w_in"])[0],
        "b_fin": np.ascontiguousarray(b_in[None, 928:1440]), "cols": cols,
        "w_q_up": f(inp["w_q_up"])[0], "w_kv_up": f(inp["w_kv_up"])[0], "w_fourier": f(inp["w_fourier"])[0],
        "w_out": f(inp["w_out"])[0], "b_out": f(inp["b_out"]), "post_ln_g": f(inp["post_ln_g"]),
        "post_ln_b": f(inp["post_ln_b"]),
    }
    shared.update(k)
    maps = []
    for i in range(ncores):
        cv = np.zeros((3, D), np.float32)
        cv[0:NB] = c[i * NB:(i + 1) * NB]
        cv[2] = c_ctx
        cT = np.ascontiguousarray(cv.reshape(3, 8, 128).transpose(2, 1, 0).reshape(128, 24))
        m = dict(shared)
        m["x"] = np.ascontiguousarray(x[i * NB:(i + 1) * NB])
        m["ctx"] = np.ascontiguousarray(ctx[i * NB:(i + 1) * NB])
        m["cT"] = cT
        maps.append(m)
    return maps


def kernel(**inputs):
    NB, ncores = 2, 8
    nc = build(NB)
    maps = _in_maps(inputs, NB, ncores)
    res = run_bass_kernel_spmd(nc, maps, core_ids=list(range(ncores)))
    return np.concatenate([np.asarray(r["out"], dtype=np.float32) for r in res.results], axis=0)
```
